# Optimizing a Trainium2 kernel written in Bass

```python
import math
import jax, jax.numpy as jnp
from jax import lax
import numpy as np

D_MODEL = 2048
BATCH = 2
SEQ = 16384
DEPTH = 1

HEAD_DIM = 128
N_FOX_HEADS = 8
N_DSA_HEADS = 8
FOX_W = N_FOX_HEADS * HEAD_DIM
DSA_W = N_DSA_HEADS * HEAD_DIM
KV_LORA = 256
N_IDX_HEADS = 16
IDX_DIM = 64
TOPK_MAX = 256
N_BUCKETS = 32
MAX_DISTANCE = 128
D_FF = 5632
CONV_WIDTH = 3
Q_BLOCK = 128
EPS = 1e-6
NEG_INF = -1e30

PROJ_SIZES = (FOX_W, FOX_W, FOX_W, N_FOX_HEADS, DSA_W, KV_LORA, N_IDX_HEADS * IDX_DIM, IDX_DIM,
              N_IDX_HEADS, D_MODEL, D_MODEL)
PROJ_WIDTH = 3 * FOX_W + N_FOX_HEADS + DSA_W + KV_LORA + N_IDX_HEADS * IDX_DIM + IDX_DIM + N_IDX_HEADS + 2 * D_MODEL

kernel_name = 'hybrid_fox_dsa_convffn_adaln'


def rms_norm(x, g):
    xf = x.astype(jnp.float32)
    y = xf * lax.rsqrt(jnp.mean(xf * xf, axis=-1, keepdims=True) + EPS)
    return (y * g).astype(x.dtype)


def _split_cols(t, sizes):
    out, start = [], 0
    for s in sizes:
        out.append(t[..., start:start + s])
        start += s
    return out


def t5_bucket(n):
    n = jnp.maximum(n, 0)
    max_exact = N_BUCKETS // 2
    nf = jnp.maximum(n, 1).astype(jnp.float32)
    large = max_exact + (jnp.log(nf / max_exact) / math.log(MAX_DISTANCE / max_exact)
                         * (N_BUCKETS - max_exact)).astype(jnp.int32)
    large = jnp.minimum(large, N_BUCKETS - 1)
    return jnp.where(n < max_exact, n, large)


def fox_attention(q, k, v, logf):
    B, L, H, Dh = q.shape
    cum_t = lax.cumsum(logf, axis=1).transpose(0, 2, 1)
    kpos = jnp.arange(L)
    scale = Dh ** -0.5

    def block(i):
        s0 = i * Q_BLOCK
        qb = lax.dynamic_slice_in_dim(q, s0, Q_BLOCK, axis=1)
        cb = lax.dynamic_slice_in_dim(cum_t, s0, Q_BLOCK, axis=2)
        logits = jnp.einsum('bqhd,bkhd->bhqk', qb, k).astype(jnp.float32) * scale
        logits = logits + cb[..., None] - cum_t[:, :, None, :]
        qpos = s0 + jnp.arange(Q_BLOCK)
        mask = kpos[None, :] <= qpos[:, None]
        logits = jnp.where(mask, logits, NEG_INF)
        p = jax.nn.softmax(logits, axis=-1)
        return jnp.einsum('bhqk,bkhd->bqhd', p.astype(v.dtype), v)

    out = lax.map(block, jnp.arange(L // Q_BLOCK))
    return out.transpose(1, 0, 2, 3, 4).reshape(B, L, H, Dh)


def dsa_attention(q, k, v, q_idx, k_idx, w_idx, rel_bias, topk):
    B, L, H, Dh = q.shape
    kpos = jnp.arange(L)
    scale = Dh ** -0.5
    idx_scale = IDX_DIM ** -0.5
    gather = jax.vmap(lambda kk, ii: kk[ii])

    def block(i):
        s0 = i * Q_BLOCK
        qb = lax.dynamic_slice_in_dim(q, s0, Q_BLOCK, axis=1)
        qib = lax.dynamic_slice_in_dim(q_idx, s0, Q_BLOCK, axis=1)
        wb = lax.dynamic_slice_in_dim(w_idx, s0, Q_BLOCK, axis=1)
        qpos = s0 + jnp.arange(Q_BLOCK)
        rel = jax.nn.relu(jnp.einsum('bqjd,bkd->bqjk', qib, k_idx) * idx_scale)
        scores = jnp.einsum('bqj,bqjk->bqk', wb, rel).astype(jnp.float32)
        causal = kpos[None, :] <= qpos[:, None]
        scores = jnp.where(causal[None], scores, NEG_INF)
        _, sel = lax.top_k(scores, topk)
        valid = sel <= qpos[None, :, None]
        ks = gather(k, sel)
        vs = gather(v, sel)
        logits = jnp.einsum('bqhd,bqkhd->bhqk', qb, ks).astype(jnp.float32) * scale
        bias = rel_bias[t5_bucket(qpos[None, :, None] - sel)].astype(jnp.float32)
        logits = logits + bias.transpose(0, 3, 1, 2)
        logits = jnp.where(valid[:, None], logits, NEG_INF)
        p = jax.nn.softmax(logits, axis=-1)
        return jnp.einsum('bhqk,bqkhd->bqhd', p.astype(vs.dtype), vs)

    out = lax.map(block, jnp.arange(L // Q_BLOCK))
    return out.transpose(1, 0, 2, 3, 4).reshape(B, L, H, Dh)


def causal_dwconv(u, w, b):
    C = u.shape[-1]
    y = lax.conv_general_dilated(u, w[:, None, :], window_strides=(1,), padding=[(CONV_WIDTH - 1, 0)],
                                 dimension_numbers=('NWC', 'WIO', 'NWC'), feature_group_count=C)
    return y + b


def setup_inputs(seed: int = 0) -> dict:
    key = jax.random.key(seed)
    ks = jax.random.split(key, 24)

    def nrm(k, shape, scale):
        return jax.random.normal(k, shape, jnp.float32) * scale

    def gain(k, shape):
        return 1.0 + 0.02 * jax.random.normal(k, shape, jnp.float32)

    return {
        'x': nrm(ks[0], (BATCH, SEQ, D_MODEL), 1.0),
        'c': nrm(ks[1], (BATCH, D_MODEL), 1.0),
        'w_ada': nrm(ks[2], (DEPTH, D_MODEL, 6 * D_MODEL), D_MODEL ** -0.5),
        'b_ada': nrm(ks[3], (DEPTH, 6 * D_MODEL), 0.02),
        'norm1_g': gain(ks[4], (DEPTH, D_MODEL)),
        'w_in': nrm(ks[5], (DEPTH, D_MODEL, PROJ_WIDTH), D_MODEL ** -0.5),
        'b_forget': 3.0 + nrm(ks[6], (DEPTH, N_FOX_HEADS), 0.5),
        'q_norm_fox': gain(ks[7], (DEPTH, HEAD_DIM)),
        'k_norm_fox': gain(ks[8], (DEPTH, HEAD_DIM)),
        'kv_norm_g': gain(ks[9], (DEPTH, KV_LORA)),
        'w_ukv': nrm(ks[10], (DEPTH, KV_LORA, 2 * DSA_W), KV_LORA ** -0.5),
        'q_norm_dsa': gain(ks[11], (DEPTH, HEAD_DIM)),
        'k_norm_dsa': gain(ks[12], (DEPTH, HEAD_DIM)),
        'w_out_fox': nrm(ks[13], (DEPTH, FOX_W, D_MODEL), FOX_W ** -0.5),
        'w_out_dsa': nrm(ks[14], (DEPTH, DSA_W, D_MODEL), DSA_W ** -0.5),
        'w_out': nrm(ks[15], (DEPTH, D_MODEL, D_MODEL), D_MODEL ** -0.5),
        'norm2_g': gain(ks[16], (DEPTH, D_MODEL)),
        'w_ffn_in': nrm(ks[17], (DEPTH, D_MODEL, 2 * D_FF), D_MODEL ** -0.5),
        'conv_w': nrm(ks[18], (DEPTH, CONV_WIDTH, 2 * D_FF), CONV_WIDTH ** -0.5),
        'conv_b': nrm(ks[19], (DEPTH, 2 * D_FF), 0.02),
        'w_ffn_out': nrm(ks[20], (DEPTH, D_FF, D_MODEL), D_FF ** -0.5),
        'rel_bias': nrm(ks[21], (N_BUCKETS, N_DSA_HEADS), 0.5),
    }


def reference(x, c, w_ada, b_ada, norm1_g, w_in, b_forget, q_norm_fox, k_norm_fox, kv_norm_g, w_ukv,
              q_norm_dsa, k_norm_dsa, w_out_fox, w_out_dsa, w_out, norm2_g, w_ffn_in, conv_w, conv_b,
              w_ffn_out, rel_bias):
    B, L, _ = x.shape
    topk = min(TOPK_MAX, L // 4)
    c_act = jax.nn.silu(c)
    for l in range(DEPTH):
        mod = c_act @ w_ada[l] + b_ada[l]
        sh1, sc1, g1, sh2, sc2, g2 = [m[:, None, :] for m in jnp.split(mod, 6, axis=-1)]

        h = rms_norm(x, norm1_g[l]) * (1 + sc1) + sh1
        proj = h @ w_in[l]
        qf, kf, vf, fgate, qd, ckv, qi, ki, wi, ga, gb = _split_cols(proj, PROJ_SIZES)

        qf = rms_norm(qf.reshape(B, L, N_FOX_HEADS, HEAD_DIM), q_norm_fox[l])
        kf = rms_norm(kf.reshape(B, L, N_FOX_HEADS, HEAD_DIM), k_norm_fox[l])
        vf = vf.reshape(B, L, N_FOX_HEADS, HEAD_DIM)
        logf = jax.nn.log_sigmoid(fgate.astype(jnp.float32) + b_forget[l].astype(jnp.float32))
        y_fox = fox_attention(qf, kf, vf, logf).reshape(B, L, FOX_W) @ w_out_fox[l]

        ckv = rms_norm(ckv, kv_norm_g[l])
        kd, vd = jnp.split(ckv @ w_ukv[l], 2, axis=-1)
        qd = rms_norm(qd.reshape(B, L, N_DSA_HEADS, HEAD_DIM), q_norm_dsa[l])
        kd = rms_norm(kd.reshape(B, L, N_DSA_HEADS, HEAD_DIM), k_norm_dsa[l])
        vd = vd.reshape(B, L, N_DSA_HEADS, HEAD_DIM)
        qi = qi.reshape(B, L, N_IDX_HEADS, IDX_DIM)
        wi = wi * (N_IDX_HEADS ** -0.5)
        y_dsa = dsa_attention(qd, kd, vd, qi, ki, wi, rel_bias, topk).reshape(B, L, DSA_W) @ w_out_dsa[l]

        merged = jax.nn.sigmoid(ga) * y_fox + jax.nn.sigmoid(gb) * y_dsa
        x = x + g1 * (merged @ w_out[l])

        h = rms_norm(x, norm2_g[l]) * (1 + sc2) + sh2
        u = causal_dwconv(h @ w_ffn_in[l], conv_w[l], conv_b[l])
        a, b = jnp.split(u, 2, axis=-1)
        x = x + g2 * ((jax.nn.silu(a) * b) @ w_ffn_out[l])
    return x
```

```python
import contextlib
import math
import os
import numpy as np
import ml_dtypes
import concourse.bass as bass
import concourse.mybir as mybir
from concourse.bass_utils import run_bass_kernel_spmd

F32 = mybir.dt.float32
BF16 = mybir.dt.bfloat16
I32 = mybir.dt.int32
AF = mybir.ActivationFunctionType
ALU = mybir.AluOpType
AX = mybir.AxisListType

D = 2048
DC = 16
L = 16384
NKB = 128
NSLOT = 8
NT = 33
NTOK = NT * 128
H = 8
DH = 128
KVL = 256
NIDX = 16
IDXD = 64
DFF = 5632
FC = 44
TOPK = 256
EPS = 1e-6
NEG = -30000.0
NBIS = 14
SCALE = DH ** -0.5
WSCALE = (IDXD ** -0.5) * (NIDX ** -0.5)

C_QF, C_KF, C_VF, C_FG, C_QD, C_CKV, C_QI, C_KI, C_WI, C_GA, C_GB = (
    0, 1024, 2048, 3072, 3080, 4104, 4360, 5384, 5448, 5464, 7512)


class _Op:
    __slots__ = ("eng", "st", "fn", "dsem", "waits_d", "waits_e", "inc", "cnt", "is_dma", "order")

    def __init__(self, eng, st, fn, dsem):
        self.eng = eng
        self.st = st
        self.fn = fn
        self.dsem = dsem
        self.is_dma = dsem is not None
        self.waits_d = {}
        self.waits_e = {}
        self.inc = False
        self.cnt = 0


class Prog:
    def __init__(self, nc):
        self.nc = nc
        self.streams = {"pe": [], "act": [], "dve": [], "pool": [], "sp": []}
        self.last_w = {}
        self.readers = {}
        self.key2phys = {}
        self.phys_total = []
        self.free_phys = []
        self.consts = set()

    def _dep(self, o, d, kind):
        if d is o:
            return
        if d.is_dma:
            if o.waits_d.get(d.dsem, 0) < d.cnt:
                o.waits_d[d.dsem] = d.cnt
            return
        if d.st == o.st and not o.is_dma:
            if o.st == "pe":
                return
            if kind == "war":
                return
        d.inc = True
        prev = o.waits_e.get(d.st)
        if prev is None or prev.order < d.order:
            o.waits_e[d.st] = d

    def op(self, eng, fn, reads=(), writes=(), dsem=None):
        st = "pool" if eng == "pq" else eng
        o = _Op(eng, st, fn, dsem)
        lst = self.streams[st]
        o.order = len(lst)
        for k in reads:
            w = self.last_w.get(k)
            if w is not None:
                self._dep(o, w, "raw")
        for k in writes:
            w = self.last_w.get(k)
            if w is not None:
                self._dep(o, w, "waw")
            for r in self.readers.get(k, ()):
                self._dep(o, r, "war")
        if o.is_dma:
            ph = self.key2phys.get(dsem)
            if ph is None:
                if self.free_phys:
                    ph = self.free_phys.pop()
                else:
                    ph = len(self.phys_total)
                    self.phys_total.append(0)
                self.key2phys[dsem] = ph
            self.phys_total[ph] += 1
            o.dsem = ph
            o.cnt = self.phys_total[ph]
        for k in reads:
            if k in self.consts:
                continue
            self.readers.setdefault(k, []).append(o)
        for k in writes:
            self.last_w[k] = o
            self.readers[k] = []
        lst.append(o)
        return o

    def barrier_dma(self, eng, dsems=None):
        st = "pool" if eng == "pq" else eng
        o = _Op(eng, st, None, None)
        o.order = len(self.streams[st])
        for ph, c in enumerate(self.phys_total):
            if c:
                o.waits_d[ph] = c
        self.streams[st].append(o)
        return o

    def full_barrier(self):
        lasts = {}
        for st in ("pe", "act", "dve", "pool"):
            for o in reversed(self.streams[st]):
                if (not o.is_dma) and o.fn is not None:
                    lasts[st] = o
                    break
        for st in self.streams:
            b = _Op(st, st, None, None)
            b.order = len(self.streams[st])
            for st2, d in lasts.items():
                if st2 != st:
                    d.inc = True
                    b.waits_e[st2] = d
            for ph, c in enumerate(self.phys_total):
                if c:
                    b.waits_d[ph] = c
            self.streams[st].append(b)
        self.free_phys = [ph for ph in range(len(self.phys_total)) if ph not in self.free_phys] + self.free_phys
        self.key2phys = {}

    def emit(self):
        nc = self.nc
        for st, ops in self.streams.items():
            c = 0
            for o in ops:
                if (not o.is_dma) and o.inc:
                    c += 1
                    o.cnt = c
        dsem_keys = list(range(len(self.phys_total)))
        self.n_ins = 0
        with contextlib.ExitStack() as es:
            esem = {st: es.enter_context(nc.semaphore("e_" + st)) for st in ("pe", "act", "dve", "pool")}
            dsem = {k: es.enter_context(nc.semaphore("d_%d" % i)) for i, k in enumerate(dsem_keys)}
            block = es.enter_context(nc.Block())

            def run(st, eng):
                waited = {}
                for o in self.streams[st]:
                    for k, v in o.waits_d.items():
                        key = ("d", k)
                        val = v * 16
                        if waited.get(key, 0) >= val:
                            continue
                        waited[key] = val
                        eng.wait_ge(dsem[k], val)
                    for k, d in o.waits_e.items():
                        key = ("e", k)
                        val = d.cnt
                        if waited.get(key, 0) >= val:
                            continue
                        waited[key] = val
                        eng.wait_ge(esem[k], val)
                    if o.fn is None:
                        continue
                    ins = o.fn(eng)
                    self.n_ins += 1
                    if o.is_dma:
                        ins.then_inc(dsem[o.dsem], 16)
                    elif o.inc:
                        ins.then_inc(esem[st], 1)

            @block.tensor
            def _(e):
                run("pe", e)

            @block.scalar
            def _(e):
                run("act", e)

            @block.vector
            def _(e):
                run("dve", e)

            @block.gpsimd
            def _(e):
                run("pool", e)

            @block.sync
            def _(e):
                run("sp", e)


class Builder:
    def __init__(self, nc, dbg=None, stop_after=None, ng_a=32):
        self.nc = nc
        self.P = Prog(nc)
        self.dbg = dbg or {}
        self.stop_after = stop_after
        self.ng_a = ng_a
        self.rr = {}

    def dram_in(self, name, shape, dt=F32):
        return self.nc.dram_tensor(name, list(shape), dt, kind="ExternalInput").ap()

    def dram_out(self, name, shape, dt=F32):
        return self.nc.dram_tensor(name, list(shape), dt, kind="ExternalOutput").ap()

    def dram_tmp(self, name, shape, dt):
        return self.nc.dram_tensor(name, list(shape), dt, kind="Internal").ap()

    def init_arena(self, es, nwords=53000):
        self.arena = es.enter_context(self.nc.sbuf_tensor("arena", [128, nwords], F32))
        self.arena_words = nwords
        self.arena_off = 0
        self.arena_peak = 0
        self._marks = {}

    def _release(self, mark):
        self.P.full_barrier()
        self.arena_off = mark

    def sb(self, es, name, shape, dt):
        if not hasattr(es, "_mk_mark"):
            mark = self.arena_off
            es._mk_mark = mark
            es.callback(self._release, mark)
        esz = {F32: 4, BF16: 2, I32: 4}[dt]
        n = 1
        for d_ in shape[1:]:
            n *= d_
        words = (n * esz + 3) // 4
        if self.arena_off + words > self.arena_words:
            raise RuntimeError("SBUF arena overflow at %s: need %d words at off %d" % (name, words, self.arena_off))
        v = self.arena[0:shape[0], self.arena_off:self.arena_off + words]
        self.arena_off += words
        self.arena_peak = max(self.arena_peak, self.arena_off)
        if dt != F32:
            v = v.bitcast(dt)
        v = v[:, 0:n]
        if len(shape) == 3:
            v = v.rearrange("p (a b) -> p a b", b=shape[2])
        elif len(shape) == 4:
            v = v.rearrange("p (a b c) -> p a b c", b=shape[2], c=shape[3])
        return v

    def ps(self, es, name, shape, dt=F32):
        return es.enter_context(self.nc.psum_tensor(name, list(shape), dt))

    def ring(self, name, n):
        i = self.rr.get(name, 0)
        self.rr[name] = i + 1
        return i % n

    def dma(self, q, out, in_, reads=(), writes=(), dsem=None):
        return self.P.op(q, lambda e: e.dma_start(out=out, in_=in_), reads=reads, writes=writes, dsem=dsem)

    def mm(self, out, lhsT, rhs, start, stop, reads, writes, tile_position=None):
        if tile_position is None:
            fn = lambda e: e.matmul(out, lhsT=lhsT, rhs=rhs, start=start, stop=stop)
        else:
            fn = lambda e: e.matmul(out, lhsT=lhsT, rhs=rhs, start=start, stop=stop, tile_position=tile_position)
        return self.P.op("pe", fn, reads=reads, writes=writes)

    def tr(self, out, in_, ident, reads, writes):
        return self.P.op("pe", lambda e: e.transpose(out=out, in_=in_, identity=ident), reads=reads, writes=writes)

    def act(self, out, in_, func, reads, writes, scale=None, bias=None, accum_out=None):
        kw = {}
        if scale is not None:
            kw["scale"] = scale
        if bias is not None:
            kw["bias"] = bias
        if accum_out is not None:
            kw["accum_out"] = accum_out
        return self.P.op("act", lambda e: e.activation(out=out, in_=in_, func=func, **kw), reads=reads, writes=writes)

    def ts(self, eng, out, in0, s1, s2, op0, op1, reads, writes, accum_out=None):
        kw = {}
        if op1 is not None:
            kw["op1"] = op1
        if accum_out is not None:
            kw["accum_out"] = accum_out
        return self.P.op(eng, lambda e: e.tensor_scalar(out=out, in0=in0, scalar1=s1, scalar2=s2, op0=op0, **kw),
                         reads=reads, writes=writes)

    def tt(self, eng, out, in0, in1, op, reads, writes):
        return self.P.op(eng, lambda e: e.tensor_tensor(out=out, in0=in0, in1=in1, op=op), reads=reads, writes=writes)

    def stt(self, out, in0, scalar, in1, op0, op1, reads, writes):
        return self.P.op("dve", lambda e: e.scalar_tensor_tensor(out=out, in0=in0, scalar=scalar, in1=in1, op0=op0, op1=op1),
                         reads=reads, writes=writes)

    def cp(self, eng, out, in_, reads, writes):
        if eng == "act":
            return self.act(out, in_, AF.Copy, reads, writes)
        return self.P.op(eng, lambda e: e.tensor_copy(out=out, in_=in_), reads=reads, writes=writes)

    def recip(self, out, in_, reads, writes):
        return self.P.op("dve", lambda e: e.reciprocal(out=out, in_=in_), reads=reads, writes=writes)

    def memset(self, eng, ap, val, writes):
        return self.P.op(eng, lambda e: e.memset(ap, val), writes=writes)


def _own_groups(j):
    return [4 * s + (j if s % 2 == 0 else 3 - j) for s in range(NSLOT)]


def build_program(nc, dbg=False, stop_after="Z", ng_a=32, nqb_c=NT, nslot_d=NSLOT, nslot_f=NSLOT):
    B = Builder(nc)
    P = B.P
    es = contextlib.ExitStack()
    with es:
        xb = B.dram_in("xb", [L, D]); xo = B.dram_in("xo", [NTOK, D])
        cT = B.dram_in("cT", [128, DC])
        w_ada = B.dram_in("w_ada", [D, 6 * D]); b_ada_c = B.dram_in("b_ada_c", [128, 96]); b_ada_r = B.dram_in("b_ada_r", [1, 6 * D])
        n1c = B.dram_in("n1c", [128, DC]); n2c = B.dram_in("n2c", [128, DC])
        w_in = B.dram_in("w_in", [D, 9560])
        b_forget = B.dram_in("b_forget", [1, 8])
        gains = B.dram_in("gains", [128, 6])
        w_ukv = B.dram_in("w_ukv", [KVL, 2 * H * DH])
        w_of = B.dram_in("w_out_fox", [H * DH, D]); w_od = B.dram_in("w_out_dsa", [H * DH, D]); w_o = B.dram_in("w_out", [D, D])
        w_fi = B.dram_in("w_ffn_in", [D, 2 * DFF]); convw = B.dram_in("convw", [128, 3, 2 * FC]); convb = B.dram_in("convb", [128, 2 * FC])
        w_fo = B.dram_in("w_ffn_out", [DFF, D])
        relb = B.dram_in("relb", [1, 256])
        cval = B.dram_in("cval", [128, 8]); selo = B.dram_in("selo", [128, 8]); selv = B.dram_in("selv", [128, 64])
        hp2 = B.dram_in("hp2", [128, 1]); hposr = B.dram_in("hposr", [128, 16]); hselt = B.dram_in("hselt", [128, 512])
        hval = B.dram_in("hval", [128, 16])
        out = B.dram_out("out", [NSLOT * 512, D])

        KF_T = B.dram_tmp("KF_T", [H, 128, L], BF16); KD_T = B.dram_tmp("KD_T", [H, 128, L], BF16)
        VF = B.dram_tmp("VF", [H, 128, NKB, 128], BF16); VD = B.dram_tmp("VD", [H, 128, NKB, 128], BF16)
        KI2_T = B.dram_tmp("KI2_T", [128, L], BF16)
        QF_T = B.dram_tmp("QF_T", [H, 128, NTOK], BF16); QD_T = B.dram_tmp("QD_T", [H, 128, NTOK], BF16)
        QI_T = B.dram_tmp("QI_T", [8, 128, NTOK], BF16); WI_S = B.dram_tmp("WI_S", [NTOK, 16], F32)
        ATTF_T = B.dram_tmp("ATTF_T", [H, 128, NTOK], BF16); ATTD_T = B.dram_tmp("ATTD_T", [H, 128, NTOK], BF16)
        WQ_S = B.dram_tmp("WQ_S", [D, 3088], BF16)
        WOF_S = B.dram_tmp("WOF_S", [16, 128, 8, 128], BF16); WOD_S = B.dram_tmp("WOD_S", [16, 128, 8, 128], BF16)
        WG_S = B.dram_tmp("WG_S", [32, 128, 16, 128], BF16)
        WO_S = B.dram_tmp("WO_S", [D, D], BF16)
        WFI_S = B.dram_tmp("WFI_S", [2 * FC, 128, 16, 128], BF16)
        WFO_S = B.dram_tmp("WFO_S", [DFF, D], BF16)
        GROW_S = B.dram_tmp("GROW_S", [128, 2 * D], F32)
        CKR = B.dram_tmp("CKR", [H, 3, NKB, 128], BF16)
        dbg_out = {}

        def dbg_t(name, shape, dt=F32):
            if dbg:
                dbg_out[name] = B.dram_out("dbg_" + name, shape, dt)
            return dbg_out.get(name)

        B.init_arena(es)
        PP = [B.ps(es, "pp%d" % i, [128, 2, 512]) for i in range(2)]
        PA = B.ps(es, "pa", [128, 512]); PBK = B.ps(es, "pbk", [128, 512])
        TPB = B.ps(es, "tpb", [128, 2048], BF16)
        MMR = [(PP[0][:, 0, :], "pp0a"), (PP[0][:, 1, :], "pp0b"), (PP[1][:, 0, :], "pp1a"), (PP[1][:, 1, :], "pp1b")]

        IDB = B.sb(es, "IDB", [128, 128], BF16); IDF = B.sb(es, "IDF", [128, 128], F32)
        NEGI = B.sb(es, "NEGI", [128, 128], BF16)
        ONESB = B.sb(es, "ONESB", [128, 128], BF16); ONESF = B.sb(es, "ONESF", [128, 128], F32)
        TRIF = B.sb(es, "TRIF", [128, 128], F32)
        MODC = B.sb(es, "MODC", [128, 96], F32)
        GS = B.sb(es, "GS", [128, 2, DC], F32)
        CUML = B.sb(es, "CUML", [128, NKB, 8], F32)
        GAINS = B.sb(es, "GAINS", [128, 6], F32)
        CVAL = B.sb(es, "CVAL", [128, 8], F32); SELO = B.sb(es, "SELO", [128, 8], F32); SELV = B.sb(es, "SELV", [128, 64], F32)
        HP2 = B.sb(es, "HP2", [128, 1], F32); HPOSR = B.sb(es, "HPOSR", [128, 16], F32)
        HSELT = B.sb(es, "HSELT", [128, 512], F32); HVAL = B.sb(es, "HVAL", [128, 16], F32)
        RBR = B.sb(es, "RBR", [128, 256], F32)
        STAT = B.sb(es, "STAT", [128, 8, 4], F32)
        JUNKB = B.sb(es, "JUNKB", [128, 2048], BF16)

        for (t, src, key) in ((GAINS, gains, "GAINS"), (CVAL, cval, "CVAL"), (SELO, selo, "SELO"), (SELV, selv, "SELV"),
                              (HP2, hp2, "HP2"), (HPOSR, hposr, "HPOSR"), (HSELT, hselt, "HSELT"), (HVAL, hval, "HVAL")):
            B.dma("sp", t[:], src[:, :], writes=[key], dsem="c_" + key)
        B.dma("sp", RBR[:], relb.partition_broadcast(128)[:, 0, :], writes=["RBR"], dsem="c_RBR")

        B.memset("pool", IDF[:], 1.0, ["IDF"])
        P.op("pool", lambda e: e.affine_select(out=IDF[:], in_=IDF[:], pattern=[[-1, 128]], compare_op=ALU.is_equal, fill=0.0,
                                               base=0, channel_multiplier=1), reads=["IDF"], writes=["IDF"])
        B.cp("dve", IDB[:], IDF[:], ["IDF"], ["IDB"])
        B.ts("dve", NEGI[:], IDF[:], NEG, None, ALU.mult, None, ["IDF"], ["NEGI"])
        B.memset("pool", ONESB[:], 1.0, ["ONESB"]); B.memset("pool", ONESF[:], 1.0, ["ONESF"])
        B.memset("pool", TRIF[:], 1.0, ["TRIF"])
        P.op("pool", lambda e: e.affine_select(out=TRIF[:], in_=TRIF[:], pattern=[[1, 128]], compare_op=ALU.is_ge, fill=0.0,
                                               base=0, channel_multiplier=-1), reads=["TRIF"], writes=["TRIF"])
        with contextlib.ExitStack() as e0:
            GROW = B.sb(e0, "GROW", [128, 2, D], F32)
            P.consts.update(["IDB", "IDF", "NEGI", "ONESB", "ONESF", "TRIF", "GAINS", "CVAL", "SELO", "SELV", "HP2",
                             "HPOSR", "HSELT", "HVAL", "RBR"])

            CT = B.sb(e0, "CT", [128, DC], F32); CACT = B.sb(e0, "CACT", [128, DC], F32)
            CREP = B.sb(e0, "CREP", [128, DC, 128], F32)
            BADAC = B.sb(e0, "BADAC", [128, 96], F32); BRG = B.sb(e0, "BRG", [128, 2, D], F32)
            N12 = B.sb(e0, "N12", [128, 2, DC], F32)
            WA32 = [B.sb(e0, "WA32_%d" % i, [128, DC, 512], F32) for i in range(2)]
            WAB = WA32
            B.dma("sp", CT[:], cT[:, :], writes=["CT"], dsem="c_CT")
            B.dma("sp", BADAC[:], b_ada_c[:, :], writes=["BADAC"], dsem="c_BADAC")
            B.dma("sp", N12[:, 0, :], n1c[:, :], writes=["N12"], dsem="c_N12")
            B.dma("sp", N12[:, 1, :], n2c[:, :], writes=["N12"], dsem="c_N12")
            B.dma("sp", BRG[:, 0, :], b_ada_r[:, 2 * D:3 * D].partition_broadcast(128)[:, 0, :], writes=["BRG"], dsem="c_BRG")
            B.dma("sp", BRG[:, 1, :], b_ada_r[:, 5 * D:6 * D].partition_broadcast(128)[:, 0, :], writes=["BRG"], dsem="c_BRG")
            B.act(CACT[:], CT[:], AF.Silu, ["CT"], ["CACT"])
            for dc in range(DC):
                B.ts("dve", CREP[:, dc, :], ONESF[:], CACT[:, dc:dc + 1], None, ALU.mult, None, ["ONESF", "CACT"], ["CREP"])
            w_ada_v = w_ada.rearrange("(dc p) n -> p dc n", p=128)
            for ci in range(24):
                sl = ci % 2
                B.dma("sp", WA32[sl][:], w_ada_v[:, :, ci * 512:(ci + 1) * 512], writes=[("wa32", sl)], dsem=("wa32", sl))
                sec = ci // 4
                if sec in (2, 5):
                    gi = 0 if sec == 2 else 1
                    pst, pk = MMR[ci % 4]
                    for dc in range(DC):
                        B.mm(pst, CREP[:, dc, :], WAB[sl][:, dc, :], dc == 0, dc == DC - 1, ["CREP", ("wa32", sl)], [pk])
                    c0 = (ci % 4) * 512
                    B.tt("dve", GROW[:, gi, c0:c0 + 512], pst, BRG[:, gi, c0:c0 + 512], ALU.add, [pk, "BRG"], ["GROW"])
                else:
                    for m4 in range(4):
                        m = ci * 4 + m4
                        for dc in range(DC):
                            B.mm(PA[:, m:m + 1], WAB[sl][:, dc, m4 * 128:(m4 + 1) * 128], CACT[:, dc:dc + 1], dc == 0, dc == DC - 1,
                                 ["CACT", ("wa32", sl)], ["pa"])
            B.tt("dve", MODC[:], PA[:, 0:96], BADAC[:], ALU.add, ["pa", "BADAC"], ["MODC"])
            for i, (sc0, sh0) in enumerate(((16, 0), (64, 48))):
                B.ts("dve", GS[:, i, :], MODC[:, sc0:sc0 + 16], 1.0, None, ALU.add, None, ["MODC"], ["GS"])
                B.tt("dve", GS[:, i, :], GS[:, i, :], N12[:, i, :], ALU.mult, ["GS", "N12"], ["GS"])
            B.dma("sp", GROW_S[:, :], GROW[:].rearrange("p a b -> p (a b)"), reads=["GROW"], dsem="grow_st")
            if dbg:
                d_mod = dbg_t("modc", [128, 96]); d_grow = dbg_t("grow", [128, 2 * D])
                B.dma("sp", d_mod[:, :], MODC[:], reads=["MODC"], dsem="dbg")
                B.dma("sp", d_grow[:, :], GROW[:].rearrange("p a b -> p (a b)"), reads=["GROW"], dsem="dbg")
        P.consts.update(["MODC", "GS"])
        SH = (MODC[:, 0:16], MODC[:, 48:64])

        def norm_tile(xt_ap, xkey, which, HT, htkey, tcol0, XN, xnkey, si):
            st = STAT[:, si, :]
            sk = ("stat", si)
            B.act(JUNKB[:], xt_ap, AF.Square, [xkey], [sk], accum_out=st[:, 0:1])
            B.act(st[:, 1:2], st[:, 0:1], AF.Sqrt, [sk], [sk], scale=1.0 / D, bias=EPS)
            B.recip(st[:, 2:3], st[:, 1:2], [sk], [sk])
            B.ts("dve", XN, xt_ap, st[:, 2:3], None, ALU.mult, None, [xkey, sk], [xnkey])
            for dc in range(DC):
                hk = ("tpb", dc // 8)
                B.tr(TPB[:, dc * 128:(dc + 1) * 128], XN[:, dc * 128:(dc + 1) * 128], IDB[:], [xnkey, "IDB"], [hk])
            for dc in range(DC):
                hk = ("tpb", dc // 8)
                o_ap = HT[:, dc, tcol0:tcol0 + 128]
                if dc % 2 == 0:
                    B.act(o_ap, TPB[:, dc * 128:(dc + 1) * 128], AF.Identity, [hk, "GS", "MODC"], [htkey],
                          scale=GS[:, which, dc:dc + 1], bias=SH[which][:, dc:dc + 1])
                else:
                    B.ts("dve", o_ap, TPB[:, dc * 128:(dc + 1) * 128], GS[:, which, dc:dc + 1], SH[which][:, dc:dc + 1],
                         ALU.mult, ALU.add, [hk, "GS", "MODC"], [htkey])

        def prep_nat(src, K, N, dst, e1, name):
            kc_all = K // 128
            W32 = [B.sb(e1, name + "32_%d" % i, [128, DC, 512], F32) for i in range(2)]
            W16 = [B.sb(e1, name + "16_%d" % i, [128, DC, 512], BF16) for i in range(2)]
            srcv = src.rearrange("(k p) n -> p k n", p=128)
            dstv = dst.rearrange("(k p) n -> p k n", p=128)
            i = 0
            for k0 in range(0, kc_all, DC):
                kcb = min(DC, kc_all - k0)
                for c0 in range(0, N, 512):
                    cw = min(512, N - c0)
                    sl = i % 2
                    B.dma("sp", W32[sl][:, 0:kcb, 0:cw], srcv[:, k0:k0 + kcb, c0:c0 + cw], writes=[(name + "32", sl)], dsem=(name + "32", sl))
                    B.cp("dve" if i % 2 == 0 else "pool", W16[sl][:, 0:kcb, 0:cw], W32[sl][:, 0:kcb, 0:cw], [(name + "32", sl)], [(name + "16", sl)])
                    B.dma("pq", dstv[:, k0:k0 + kcb, c0:c0 + cw], W16[sl][:, 0:kcb, 0:cw], reads=[(name + "16", sl)], dsem=(name + "st", sl))
                    i += 1
            return [(name + "st", 0), (name + "st", 1)]

        def prep_tiled(src, K, c_lo, c_hi, dst, j_off, e1, name):
            kcb = K // 128
            W32 = [B.sb(e1, name + "32_%d" % i, [128, DC, 512], F32) for i in range(2)]
            W16 = [B.sb(e1, name + "16_%d" % i, [128, DC, 512], BF16) for i in range(2)]
            srcv = src.rearrange("(k p) n -> p k n", p=128)
            i = 0
            for c0 in range(c_lo, c_hi, 512):
                cw = min(512, c_hi - c0)
                nj = cw // 128
                sl = i % 2
                B.dma("sp", W32[sl][:, 0:kcb, 0:cw], srcv[:, :, c0:c0 + cw], writes=[(name + "32", sl)], dsem=(name + "32", sl))
                B.cp("dve" if i % 2 == 0 else "pool", W16[sl][:, 0:kcb, 0:cw], W32[sl][:, 0:kcb, 0:cw], [(name + "32", sl)], [(name + "16", sl)])
                j0 = j_off + (c0 - c_lo) // 128
                B.dma("pq", dst[j0:j0 + nj, :, :, :].rearrange("j p k c -> p k j c"),
                      W16[sl][:, 0:kcb, 0:cw].rearrange("p k (j c) -> p k j c", c=128), reads=[(name + "16", sl)], dsem=(name + "st", sl))
                i += 1
            return [(name + "st", 0), (name + "st", 1)]

        st_sems = []
        with contextlib.ExitStack() as e1:
            W32 = [B.sb(e1, "wq32_%d" % i, [128, DC, 512], F32) for i in range(2)]
            W16 = [B.sb(e1, "wq16_%d" % i, [128, DC, 512], BF16) for i in range(2)]
            w_in_v = w_in.rearrange("(k p) n -> p k n", p=128)
            wq_v = WQ_S.rearrange("(k p) n -> p k n", p=128)
            i = 0
            for (slo, shi, dlo) in ((C_QF, C_QF + 1024, 0), (C_QD, C_QD + 1024, 1024), (C_QI, C_QI + 1024, 2048), (C_WI, C_WI + 16, 3072)):
                for c0 in range(slo, shi, 512):
                    cw = min(512, shi - c0)
                    sl = i % 2
                    B.dma("sp", W32[sl][:, :, 0:cw], w_in_v[:, :, c0:c0 + cw], writes=[("wq32", sl)], dsem=("wq32", sl))
                    B.cp("dve" if i % 2 == 0 else "pool", W16[sl][:, :, 0:cw], W32[sl][:, :, 0:cw], [("wq32", sl)], [("wq16", sl)])
                    d0 = dlo + (c0 - slo)
                    B.dma("pq", wq_v[:, :, d0:d0 + cw], W16[sl][:, :, 0:cw], reads=[("wq16", sl)], dsem=("wqst", sl))
                    i += 1
            st_sems += [("wqst", 0), ("wqst", 1)]
        if stop_after >= "E":
            with contextlib.ExitStack() as e1:
                st_sems += prep_tiled(w_of, H * DH, 0, D, WOF_S, 0, e1, "wof")
            with contextlib.ExitStack() as e1:
                st_sems += prep_tiled(w_od, H * DH, 0, D, WOD_S, 0, e1, "wod")
            with contextlib.ExitStack() as e1:
                st_sems += prep_tiled(w_in, D, C_GA, C_GA + 2 * D, WG_S, 0, e1, "wg")
            with contextlib.ExitStack() as e1:
                st_sems += prep_nat(w_o, D, D, WO_S, e1, "wo")
            with contextlib.ExitStack() as e1:
                st_sems += prep_tiled(w_fi, D, 0, 2 * DFF, WFI_S, 0, e1, "wfi")
            with contextlib.ExitStack() as e1:
                st_sems += prep_nat(w_fo, DFF, D, WFO_S, e1, "wfo")

        def head_norm(e_, ps_ap, pkey, gain_ap, out_ap, okey, nfeat, ncols, SQ, SD, tag):
            i = B.ring("hn" + tag, 2)
            sq = SQ[i][:, 0:ncols]; sd = SD[i][:, 0:ncols]
            B.act(sq, ps_ap, AF.Square, [pkey], [("sq" + tag, i)])

            def post():
                B.mm(PA[:, 0:ncols], ONESB[:], sq, True, True, [("sq" + tag, i), "ONESB"], ["pa"])
                B.act(sd, PA[:, 0:ncols], AF.Sqrt, ["pa"], [("sd" + tag, i)], scale=1.0 / nfeat, bias=EPS)
                B.recip(sd, sd, [("sd" + tag, i)], [("sd" + tag, i)])
                B.stt(out_ap, ps_ap, gain_ap, sd, ALU.mult, ALU.mult, [pkey, ("sd" + tag, i), "GAINS"], [okey])
            return post

        with contextlib.ExitStack() as eA:
            WKV = B.sb(eA, "WKV", [128, DC, 2440], BF16)
            WUKV = B.sb(eA, "WUKV", [128, 2, 2048], BF16)
            with contextlib.ExitStack() as e1:
                W32 = [B.sb(e1, "wa32_%d" % i, [128, DC, 512], F32) for i in range(2)]
                i = 0
                for (slo, cw, dlos) in ((C_KF, 512, (0,)), (C_KF + 512, 512, (512,)), (C_CKV, 256, (1024,)), (C_KI, 64, (1280, 1344)),
                                        (C_VF, 512, (1408,)), (C_VF + 512, 512, (1920,)), (C_FG, 8, (2432,))):
                    sl = i % 2
                    B.dma("sp", W32[sl][:, :, 0:cw], w_in_v[:, :, slo:slo + cw], writes=[("wa32", sl)], dsem=("wa32", sl))
                    for dlo in dlos:
                        B.cp("dve" if i % 2 == 0 else "pool", WKV[:, :, dlo:dlo + cw], W32[sl][:, :, 0:cw], [("wa32", sl)], ["WKV"])
                    i += 1
                w_ukv_v = w_ukv.rearrange("(k p) n -> p k n", p=128)
                for c0 in range(0, 2048, 1024):
                    sl = i % 2
                    wv = W32[sl][:, 0:4, :].rearrange("p k n -> p (k n)").rearrange("p (k n) -> p k n", k=2)
                    B.dma("sp", wv, w_ukv_v[:, :, c0:c0 + 1024], writes=[("wa32", sl)], dsem=("wa32", sl))
                    B.cp("dve", WUKV[:, :, c0:c0 + 1024], wv, [("wa32", sl)], ["WUKV"])
                    i += 1
            P.consts.update(["WKV", "WUKV"])
            BFG = B.sb(eA, "BFG", [128, 8], F32)
            CARRY = B.sb(eA, "CARRY", [128, 2, 8], F32)
            XT = [B.sb(eA, "XT%d" % i, [128, D], F32) for i in range(2)]
            XN = [B.sb(eA, "XN%d" % i, [128, D], BF16) for i in range(2)]
            HT = [B.sb(eA, "HT%d" % i, [128, DC, 512], BF16) for i in range(2)]
            SQ = [B.sb(eA, "SQ%d" % i, [128, 512], BF16) for i in range(2)]
            SD = [B.sb(eA, "SD%d" % i, [128, 512], F32) for i in range(2)]
            CKVN = [B.sb(eA, "CKVN%d" % i, [128, 2, 512], BF16) for i in range(2)]
            KST = [B.sb(eA, "KST%d" % i, [128, 8, 512], BF16) for i in range(2)]
            KIST = [B.sb(eA, "KIST%d" % i, [128, 512], BF16) for i in range(2)]
            VST = [B.sb(eA, "VST%d" % i, [128, 1024], BF16) for i in range(2)]
            LG = [B.sb(eA, "LG%d" % i, [128, 3, 8], F32) for i in range(2)]
            B.dma("sp", BFG[:], b_forget.partition_broadcast(128)[:, 0, :], writes=["BFG"], dsem="c_BFG")
            B.memset("pool", CARRY[:, 0, :], 0.0, [("carry", 0)])
            kf_v = KF_T.rearrange("h p t -> p h t"); kd_v = KD_T.rearrange("h p t -> p h t")
            for g in range(ng_a):
                hs = B.ring("htA", 2)
                htk = ("htA", hs)
                for t in range(4):
                    kb = 4 * g + t
                    xs = B.ring("xtA", 2)
                    B.dma("sp", XT[xs][:], xb[kb * 128:(kb + 1) * 128, :], writes=[("xtA", xs)], dsem=("xtA", xs))
                    ns = B.ring("xnA", 2)
                    norm_tile(XT[xs][:], ("xtA", xs), 0, HT[hs], htk, t * 128, XN[ns], ("xnA", ns), B.ring("stat", 8))
                ks = B.ring("kstA", 2)
                pend = None
                for h in range(H):
                    pst, pk = MMR[B.ring("mmr", 4)]
                    for dc in range(DC):
                        B.mm(pst, WKV[:, dc, h * 128:(h + 1) * 128], HT[hs][:, dc, :], dc == 0, dc == DC - 1, ["WKV", htk], [pk])
                    post = head_norm(eA, pst, pk, GAINS[:, 1:2], KST[ks][:, h, :], ("kstA", ks), 128, 512, SQ, SD, "A")
                    if pend:
                        pend()
                    pend = post
                pend()
                B.dma("pq", kf_v[:, :, g * 512:(g + 1) * 512], KST[ks][:], reads=[("kstA", ks)], dsem=("kstA", ks))
                pst, pk = MMR[B.ring("mmr", 4)]
                for dc in range(DC):
                    B.mm(pst, WKV[:, dc, 1280:1408], HT[hs][:, dc, :], dc == 0, dc == DC - 1, ["WKV", htk], [pk])
                kis = B.ring("kistA", 2)
                B.cp("act", KIST[kis][:], pst, [pk], [("kistA", kis)])
                B.dma("pq", KI2_T[:, g * 512:(g + 1) * 512], KIST[kis][:], reads=[("kistA", kis)], dsem=("kistA", kis))
                cs = B.ring("ckvnA", 2)
                ckk = ("ckvnA", cs)
                cps = []
                for fc in range(2):
                    pst, pk = MMR[B.ring("mmr", 4)]
                    for dc in range(DC):
                        B.mm(pst, WKV[:, dc, 1024 + fc * 128:1024 + (fc + 1) * 128], HT[hs][:, dc, :], dc == 0, dc == DC - 1, ["WKV", htk], [pk])
                    cps.append((pst, pk))
                for fc in range(2):
                    qi_ = B.ring("hnA", 2)
                    B.act(SQ[qi_][:], cps[fc][0], AF.Square, [cps[fc][1]], [("sqA", qi_)])
                    B.mm(PA[:, :], ONESB[:], SQ[qi_][:], fc == 0, fc == 1, [("sqA", qi_), "ONESB"], ["pa"])
                B.act(SD[qi_][:], PA[:, :], AF.Sqrt, ["pa"], [("sdA", qi_)], scale=1.0 / KVL, bias=EPS)
                B.recip(SD[qi_][:], SD[qi_][:], [("sdA", qi_)], [("sdA", qi_)])
                for fc in range(2):
                    B.stt(CKVN[cs][:, fc, :], cps[fc][0], GAINS[:, 4 + fc:5 + fc], SD[qi_][:], ALU.mult, ALU.mult,
                          [cps[fc][1], ("sdA", qi_), "GAINS"], [ckk])
                ks = B.ring("kstA", 2)
                pend = None
                for h in range(H):
                    pst, pk = MMR[B.ring("mmr", 4)]
                    for fc in range(2):
                        B.mm(pst, WUKV[:, fc, h * 128:(h + 1) * 128], CKVN[cs][:, fc, :], fc == 0, fc == 1, ["WUKV", ckk], [pk])
                    post = head_norm(eA, pst, pk, GAINS[:, 3:4], KST[ks][:, h, :], ("kstA", ks), 128, 512, SQ, SD, "A")
                    if pend:
                        pend()
                    pend = post
                pend()
                B.dma("pq", kd_v[:, :, g * 512:(g + 1) * 512], KST[ks][:], reads=[("kstA", ks)], dsem=("kstA", ks))
                for t in range(4):
                    kb = 4 * g + t
                    tc = slice(t * 128, (t + 1) * 128)
                    vs = B.ring("vstA", 2)
                    for cc in range(2):
                        pst, pk = MMR[B.ring("mmr", 4)]
                        for dc in range(DC):
                            B.mm(pst, HT[hs][:, dc, tc], WKV[:, dc, 1408 + cc * 512:1408 + (cc + 1) * 512], dc == 0, dc == DC - 1, ["WKV", htk], [pk])
                        B.cp("act" if cc == 0 else "dve", VST[vs][:, cc * 512:(cc + 1) * 512], pst, [pk], [("vstA", vs)])
                    B.dma("pq", VF[:, :, kb, :].rearrange("h p d -> p h d"), VST[vs][:].rearrange("p (h d) -> p h d", d=128),
                          reads=[("vstA", vs)], dsem=("vstA", vs))
                    vs = B.ring("vstA", 2)
                    for cc in range(2):
                        pst, pk = MMR[B.ring("mmr", 4)]
                        for fc in range(2):
                            B.mm(pst, CKVN[cs][:, fc, tc], WUKV[:, fc, 1024 + cc * 512:1024 + (cc + 1) * 512], fc == 0, fc == 1, ["WUKV", ckk], [pk])
                        B.cp("act" if cc == 0 else "dve", VST[vs][:, cc * 512:(cc + 1) * 512], pst, [pk], [("vstA", vs)])
                    B.dma("pq", VD[:, :, kb, :].rearrange("h p d -> p h d"), VST[vs][:].rearrange("p (h d) -> p h d", d=128),
                          reads=[("vstA", vs)], dsem=("vstA", vs))
                    for dc in range(DC):
                        B.mm(PBK[:, 0:8], HT[hs][:, dc, tc], WKV[:, dc, 2432:2440], dc == 0, dc == DC - 1, ["WKV", htk], ["pbk"])
                    ls = B.ring("lgA", 2)
                    lk = ("lgA", ls)
                    B.tt("dve", LG[ls][:, 0, :], PBK[:, 0:8], BFG[:], ALU.add, ["pbk", "BFG"], [lk])
                    B.act(LG[ls][:, 1, :], LG[ls][:, 0, :], AF.Exp, [lk], [lk], scale=-1.0)
                    B.act(LG[ls][:, 2, :], LG[ls][:, 1, :], AF.Ln, [lk], [lk], bias=1.0)
                    B.mm(PBK[:, 8:16], TRIF[:], LG[ls][:, 2, :], True, True, ["TRIF", lk], ["pbk"])
                    B.mm(PBK[:, 16:24], ONESF[:], LG[ls][:, 2, :], True, True, ["ONESF", lk], ["pbk"])
                    cur = kb % 2
                    B.tt("dve", CUML[:, kb, :], PBK[:, 8:16], CARRY[:, cur, :], ALU.add, ["pbk", ("carry", cur)], ["CUML"])
                    B.tt("dve", CARRY[:, 1 - cur, :], PBK[:, 16:24], CARRY[:, cur, :], ALU.add, ["pbk", ("carry", cur)], [("carry", 1 - cur)])
            if dbg:
                d_cum = dbg_t("cuml", [128, NKB * 8])
                B.dma("sp", d_cum[:, :], CUML[:].rearrange("p a b -> p (a b)"), reads=["CUML"], dsem="dbg")
        P.consts.update(["CUML"])
        if dbg:
            for nm, src, shp in (("kf", KF_T[0, :, 0:1024], [128, 1024]), ("kd", KD_T[7, :, 0:1024], [128, 1024]),
                                 ("ki", KI2_T[:, 0:1024], [128, 1024])):
                d_ = dbg_t(nm, shp, BF16)
                B.dma("sp", d_[:, :], src, dsem="dbg")
            for nm, src in (("vf", VF[1, :, 0:8, :]), ("vd", VD[6, :, 0:8, :])):
                d_ = dbg_t(nm, [128, 8, 128], BF16)
                B.dma("sp", d_[:, :, :], src, dsem="dbg")

        P.barrier_dma("sp", st_sems)
        with contextlib.ExitStack() as eB:
            WQ = B.sb(eB, "WQ", [128, DC, 3088], BF16)
            XT = [B.sb(eB, "XTb%d" % i, [128, D], F32) for i in range(2)]
            XN = [B.sb(eB, "XNb%d" % i, [128, D], BF16) for i in range(2)]
            HT = [B.sb(eB, "HTb%d" % i, [128, DC, 512], BF16) for i in range(2)]
            SQ = [B.sb(eB, "SQb%d" % i, [128, 512], BF16) for i in range(2)]
            SD = [B.sb(eB, "SDb%d" % i, [128, 512], F32) for i in range(2)]
            QST = [B.sb(eB, "QST%d" % i, [128, 8, 512], BF16) for i in range(2)]
            WIST = [B.sb(eB, "WIST%d" % i, [128, 16], F32) for i in range(2)]
            for k4 in range(4):
                B.dma("sp", WQ[:, k4 * 4:(k4 + 1) * 4, :], wq_v[:, k4 * 4:(k4 + 1) * 4, :], writes=["WQ"], dsem="c_WQ")
            P.consts.add("WQ")
            for gi in range(NSLOT + 1):
                ntile = 4 if gi < NSLOT else 1
                ncols = ntile * 128
                tok0 = gi * 512
                hs = B.ring("htB", 2); htk = ("htB", hs)
                for t in range(ntile):
                    xs = B.ring("xtB", 2)
                    B.dma("sp", XT[xs][:], xo[tok0 + t * 128:tok0 + (t + 1) * 128, :], writes=[("xtB", xs)], dsem=("xtB", xs))
                    ns = B.ring("xnB", 2)
                    norm_tile(XT[xs][:], ("xtB", xs), 0, HT[hs], htk, t * 128, XN[ns], ("xnB", ns), B.ring("stat", 8))
                for (seg, dst, gcol) in ((0, QF_T, 0), (1024, QD_T, 2)):
                    ks = B.ring("qstB", 2)
                    pend = None
                    for h in range(H):
                        pst, pk = MMR[B.ring("mmr", 4)]
                        for dc in range(DC):
                            B.mm(pst[:, 0:ncols], WQ[:, dc, seg + h * 128:seg + (h + 1) * 128], HT[hs][:, dc, 0:ncols], dc == 0, dc == DC - 1, ["WQ", htk], [pk])
                        post = head_norm(eB, pst[:, 0:ncols], pk, GAINS[:, gcol:gcol + 1], QST[ks][:, h, 0:ncols], ("qstB", ks), 128, ncols, SQ, SD, "B")
                        if pend:
                            pend()
                        pend = post
                    pend()
                    B.dma("pq", dst.rearrange("h p t -> p h t")[:, :, tok0:tok0 + ncols], QST[ks][:, :, 0:ncols], reads=[("qstB", ks)], dsem=("qstB", ks))
                ks = B.ring("qstB", 2)
                for m in range(8):
                    pst, pk = MMR[B.ring("mmr", 4)]
                    for dc in range(DC):
                        B.mm(pst[:, 0:ncols], WQ[:, dc, 2048 + m * 128:2048 + (m + 1) * 128], HT[hs][:, dc, 0:ncols], dc == 0, dc == DC - 1, ["WQ", htk], [pk])
                    B.cp("act" if m % 2 == 0 else "dve", QST[ks][:, m, 0:ncols], pst[:, 0:ncols], [pk], [("qstB", ks)])
                B.dma("pq", QI_T.rearrange("h p t -> p h t")[:, :, tok0:tok0 + ncols], QST[ks][:, :, 0:ncols], reads=[("qstB", ks)], dsem=("qstB", ks))
                for t in range(ntile):
                    for dc in range(DC):
                        B.mm(PBK[:, 0:16], HT[hs][:, dc, t * 128:(t + 1) * 128], WQ[:, dc, 3072:3088], dc == 0, dc == DC - 1, ["WQ", htk], ["pbk"])
                    ws = B.ring("wistB", 2)
                    B.ts("dve", WIST[ws][:], PBK[:, 0:16], WSCALE, None, ALU.mult, None, ["pbk"], [("wistB", ws)])
                    B.dma("pq", WI_S[tok0 + t * 128:tok0 + (t + 1) * 128, :], WIST[ws][:], reads=[("wistB", ws)], dsem=("wistB", ws))
        if dbg:
            for nm, src in (("qf", QF_T[3, :, 0:1024]), ("qd", QD_T[5, :, NSLOT * 512 - 896:NSLOT * 512 + 128]), ("qi", QI_T[2, :, 0:1024])):
                d_ = dbg_t(nm, [128, 1024], BF16)
                B.dma("sp", d_[:, :], src, dsem="dbg")
            d_ = dbg_t("wi", [1024, 16], F32)
            B.dma("sp", d_[:, :], WI_S[0:1024, :], dsem="dbg")

        def t5_starts():
            n = np.arange(0, 400)
            nf = np.maximum(n, 1).astype(np.float32)
            large = 16 + (np.log(nf / np.float32(16)) / np.float32(math.log(128 / 16)) * np.float32(16)).astype(np.int32)
            large = np.minimum(large, 31)
            bk = np.where(n < 16, n, large)
            return [int(np.min(n[bk == b_])) for b_ in range(32)]
        STARTS = t5_starts()

        def qb_info(qb):
            if qb < 32:
                s_ = qb // 4
                return dict(s=s_, c=qb % 4, par=s_ % 2, nkb=16 * s_ + 16, vs=16 * s_, ncols=128, halo=False)
            return dict(s=None, c=None, par=None, nkb=NKB, vs=0, ncols=16, halo=True)

        CH = 8

        if stop_after >= "C":
          with contextlib.ExitStack() as eC:
            IOKP = B.sb(eC, "IOKP", [128, 2048], F32)
            SELH = B.sb(eC, "SELH", [128, 8, 128], BF16)
            BT = B.sb(eC, "BT", [128, H, 2, 128], BF16)
            HBP = B.sb(eC, "HBP", [128, H, NSLOT, 16], BF16)
            POW2 = B.sb(eC, "POW2", [128, NBIS], F32)
            with contextlib.ExitStack() as e1:
                IOI = B.sb(e1, "IOI", [128, 2048], I32)
                BTF = B.sb(e1, "BTF", [128, H, 2, 128], F32)
                GT_ = B.sb(e1, "GT_", [128, 128], F32)
                DELTA = B.sb(e1, "DELTA", [128, 248], F32)
                P.op("pool", lambda e: e.iota(IOI[:], pattern=[[1, 2048]], base=0, channel_multiplier=-1), writes=["IOI"])
                B.cp("dve", IOKP[:], IOI[:], ["IOI"], ["IOKP"])
                for i in range(8):
                    B.ts("dve", SELH[:, i, :], IDB[:], SELO[:, i:i + 1], None, ALU.mult, None, ["IDB", "SELO"], ["SELH"])
                for i in range(NBIS):
                    B.memset("pool", POW2[:, i:i + 1], 2.0 ** -(i + 1), ["POW2"])
                B.tt("dve", DELTA[:], RBR[:, 0:248], RBR[:, 8:256], ALU.subtract, ["RBR"], ["DELTA"])
                B.memset("pool", BTF[:], 0.0, ["BTF"])
                for dl in range(2):
                    for b_ in range(31):
                        B.ts("dve", GT_[:], IOKP[:, 0:128], float(STARTS[b_ + 1] - 128 * dl), None, ALU.is_lt, None, ["IOKP"], ["GT_"])
                        for h in range(H):
                            B.stt(BTF[:, h, dl, :], GT_[:], DELTA[:, b_ * 8 + h:b_ * 8 + h + 1], BTF[:, h, dl, :], ALU.mult, ALU.add,
                                  ["GT_", "DELTA", "BTF"], ["BTF"])
                B.cp("dve", BT[:], BTF[:], ["BTF"], ["BT"])
                B.memset("pool", HBP[:], 0.0, ["HBP"])
                for s_ in range(NSLOT):
                    B.cp("dve", HBP[:, :, s_, 2 * s_:2 * s_ + 2], BT[:, :, 0, 126:128], ["BT", "HBP"], ["HBP"])
            P.consts.update(["IOKP", "SELH", "BT", "HBP", "POW2"])

            with contextlib.ExitStack() as eD:
                SC = B.sb(eD, "SC", [128, L], F32)
                NM = B.sb(eD, "NM", [128, L], BF16)
                MT = B.sb(eD, "MT", [128, L], BF16)
                QI = [B.sb(eD, "QI%d" % i, [128, 8, 128], BF16) for i in range(2)]
                WIQ = [B.sb(eD, "WIQ%d" % i, [128, 16], F32) for i in range(2)]
                DIAG = B.sb(eD, "DIAG", [128, 16, 128], BF16)
                KIC = [B.sb(eD, "KIC%d" % i, [128, 2048], BF16) for i in range(2)]
                RR = [B.sb(eD, "RR%d" % i, [128, 1024], BF16) for i in range(3)]
                BIS = B.sb(eD, "BIS", [128, 8], F32)
                BW = B.sb(eD, "BW", [128, NBIS], F32)
                QT = [B.sb(eD, "QT%d" % i, [128, 8, 128], BF16) for i in range(2)]
                KC = [B.sb(eD, "KC%d" % i, [128, CH * 128], BF16) for i in range(2)]
                VC = [B.sb(eD, "VC%d" % i, [128, CH, 130], BF16) for i in range(2)]
                PT = [B.sb(eD, "PT%d" % i, [128, 512], BF16) for i in range(3)]
                OS = [B.sb(eD, "OS%d" % i, [128, 128], BF16) for i in range(2)]
                RC = [B.sb(eD, "RC%d" % i, [128, 1], F32) for i in range(2)]
                OT = [B.sb(eD, "OT%d" % i, [128, 8, 128], BF16) for i in range(2)]
                for i in range(2):
                    B.memset("pool", VC[i][:, :, 128:129], 1.0, [("vcC1", i)])
                qi_v = QI_T.rearrange("m p t -> p m t"); qd_v = QD_T.rearrange("h p t -> p h t"); attd_v = ATTD_T.rearrange("h p t -> p h t")
                qb_list = list(range(nqb_c)) if nqb_c < NT else list(range(NT))
                if nqb_c < NT and (NT - 1) not in qb_list:
                    qb_list.append(NT - 1)
                PPX = [(PP[0], ("pp0a", "pp0b")), (PP[1], ("pp1a", "pp1b"))]

                class Ctx:
                    pass

                def c_load(qb):
                    cx = Ctx()
                    cx.qb = qb
                    cx.inf = qb_info(qb)
                    cx.nkb, cx.vs, cx.ncols = cx.inf["nkb"], cx.inf["vs"], cx.inf["ncols"]
                    cx.nk = cx.nkb * 128
                    cx.qs = B.ring("qiC", 2)
                    qs = cx.qs
                    B.dma("sp", QI[qs][:], qi_v[:, :, qb * 128:(qb + 1) * 128], writes=[("qiC", qs)], dsem=("qiC", qs))
                    B.dma("sp", WIQ[qs][:], WI_S[qb * 128:(qb + 1) * 128, :], writes=[("wiqC", qs)], dsem=("wiqC", qs))
                    for j_ in range(16):
                        B.ts("dve", DIAG[:, j_, :], IDB[:], WIQ[qs][:, j_:j_ + 1], None, ALU.mult, None, ["IDB", ("wiqC", qs)], ["DIAG"])
                    return cx

                def c1(cx):
                    qs = cx.qs
                    kic_of = {}

                    def c1_pair(kt, m):
                        if kt % 4 == 0 and m == 0:
                            kis_ = B.ring("kicC", 2)
                            B.dma("sp", KIC[kis_][:], KI2_T[:, kt * 512:kt * 512 + 2048], writes=[("kicC", kis_)], dsem=("kicC", kis_))
                            kic_of[kt // 4] = kis_
                        kis_ = kic_of[kt // 4]
                        kc0 = (kt % 4) * 512
                        ppt, pkk = PPX[B.ring("ppC", 2)]
                        B.mm(ppt[:, 0, :], QI[qs][0:64, m, :], KIC[kis_][0:64, kc0:kc0 + 512], True, True, [("qiC", qs), ("kicC", kis_)], [pkk[0]])
                        B.mm(ppt[:, 1, :], QI[qs][64:128, m, :], KIC[kis_][64:128, kc0:kc0 + 512], True, True, [("qiC", qs), ("kicC", kis_)], [pkk[1]])
                        ri = B.ring("rrC", 3)
                        ppv = ppt[:].rearrange("p a b -> p (a b)")
                        if m % 2 == 0:
                            B.act(RR[ri][:], ppv, AF.Relu, list(pkk), [("rrC", ri)])
                        else:
                            B.ts("dve", RR[ri][:], ppv, 0.0, None, ALU.max, None, list(pkk), [("rrC", ri)])
                        return ri

                    def c1_rest(kt, m, ri):
                        acc, ak = ((PA, "pa"), (PBK, "pbk"))[kt % 2]
                        B.mm(acc[:, :], DIAG[:, 2 * m, :], RR[ri][:, 0:512], m == 0, False, ["DIAG", ("rrC", ri)], [ak])
                        B.mm(acc[:, :], DIAG[:, 2 * m + 1, :], RR[ri][:, 512:1024], False, m == 7, ["DIAG", ("rrC", ri)], [ak])
                        if m == 7:
                            B.cp("act" if kt % 2 == 0 else "dve", SC[:, kt * 512:(kt + 1) * 512], acc[:, :], [ak], ["SC"])

                    queue = []
                    for kt in range(cx.nkb // 4):
                        for m in range(8):
                            ri = c1_pair(kt, m)
                            queue.append((kt, m, ri))
                            if len(queue) > 1:
                                c1_rest(*queue.pop(0))
                    for q_ in queue:
                        c1_rest(*q_)

                def c2_steps(cx):
                    inf, nk, vs = cx.inf, cx.nk, cx.vs
                    steps = []

                    def pro():
                        P.op("dve", lambda e: e.tensor_reduce(out=BIS[:, 0:1], in_=SC[:, 0:nk], axis=AX.X, op=ALU.min), reads=["SC"], writes=["BIS"])
                        if not inf["halo"]:
                            cv = CVAL[:, inf["par"] * 4 + inf["c"]:inf["par"] * 4 + inf["c"] + 1]
                            B.ts("dve", NM[:, 0:2048], IOKP[:], cv, 0.0, ALU.subtract, ALU.is_gt, ["IOKP", "CVAL", "NM"], ["NM"])
                            B.stt(SC[:, vs * 128:vs * 128 + 2048], NM[:, 0:2048], -1e30, SC[:, vs * 128:vs * 128 + 2048], ALU.mult, ALU.add, ["NM", "SC"], ["SC"])
                        else:
                            for ch in range(8):
                                B.ts("dve", NM[:, 0:2048], IOKP[:], HP2[:, 0:1], float(-ch * 2048), ALU.add, ALU.is_gt, ["IOKP", "HP2", "NM"], ["NM"])
                                B.stt(SC[:, ch * 2048:(ch + 1) * 2048], NM[:, 0:2048], -1e30, SC[:, ch * 2048:(ch + 1) * 2048], ALU.mult, ALU.add, ["NM", "SC"], ["SC"])
                        P.op("dve", lambda e: e.tensor_reduce(out=BIS[:, 1:2], in_=SC[:, 0:nk], axis=AX.X, op=ALU.max), reads=["SC", "BIS"], writes=["BIS"])
                        B.tt("dve", BIS[:, 2:3], BIS[:, 1:2], BIS[:, 0:1], ALU.subtract, ["BIS"], ["BIS"])
                        B.ts("dve", BW[:], POW2[:], BIS[:, 2:3], None, ALU.mult, None, ["POW2", "BIS"], ["BW"])
                        B.cp("dve", BIS[:, 3:4], BIS[:, 0:1], ["BIS"], ["BIS"])
                    steps.append(pro)

                    def mk_it(it):
                        def f():
                            B.tt("dve", BIS[:, 4:5], BIS[:, 3:4], BW[:, it:it + 1], ALU.add, ["BIS", "BW"], ["BIS"])
                            B.ts("dve", NM[:, 0:nk], SC[:, 0:nk], BIS[:, 4:5], 0.0, ALU.is_ge, ALU.add, ["SC", "BIS", "NM"], ["NM", "BIS"], accum_out=BIS[:, 5:6])
                            B.ts("dve", BIS[:, 6:7], BIS[:, 5:6], float(TOPK), BW[:, it:it + 1], ALU.is_ge, ALU.mult, ["BIS", "BW"], ["BIS"])
                            B.tt("dve", BIS[:, 3:4], BIS[:, 3:4], BIS[:, 6:7], ALU.add, ["BIS"], ["BIS"])
                        return f
                    for it in range(NBIS):
                        steps.append(mk_it(it))

                    def epi():
                        B.ts("dve", NM[:, 0:nk], SC[:, 0:nk], BIS[:, 3:4], None, ALU.is_lt, None, ["SC", "BIS", "NM"], ["NM"])
                    steps.append(epi)
                    return steps

                def c3(cx):
                    ncols = cx.ncols
                    if cx.inf["halo"]:
                        mtv = MT[:, 0:NKB * 16].rearrange("p (a b) -> p a b", b=16)
                    else:
                        mtv = MT[:].rearrange("p (a b) -> p a b", b=128)
                    for kb in range(cx.nkb):
                        hk = ("tpb", (kb // 8) % 2)
                        B.tr(TPB[:, (kb % 16) * 128:(kb % 16 + 1) * 128], NM[:, kb * 128:(kb + 1) * 128], IDB[:], ["NM", "IDB"], [hk])
                        if kb % 8 == 7:
                            half = (kb // 8) % 2
                            src = TPB[:, half * 1024:(half + 1) * 1024].rearrange("p (a b) -> p a b", b=128)[:, :, 0:ncols]
                            B.cp("act" if half == 0 else "dve", mtv[:, kb - 7:kb + 1, :], src, [hk], ["MT"])

                def c4_heads(cx):
                    qb, inf, nkb, vs, ncols = cx.qb, cx.inf, cx.nkb, cx.vs, cx.ncols
                    qts = B.ring("qtC", 2)
                    B.dma("sp", QT[qts][:], qd_v[:, :, qb * 128:(qb + 1) * 128], writes=[("qtC", qts)], dsem=("qtC", qts))
                    ots = B.ring("otC", 2)
                    cx.ots = ots
                    if inf["halo"]:
                        B.memset("pool", OT[ots][:], 0.0, [("otC", ots)])
                    cand = {}
                    if not inf["halo"]:
                        for dl in range(2):
                            for i in range(4):
                                kb = vs + inf["c"] - dl + 4 * i
                                if kb >= 0:
                                    cand.setdefault(kb, []).append((inf["par"] * 4 + i, dl))
                    else:
                        for s_ in range(NSLOT):
                            for i in range(4):
                                kb = 16 * s_ + 4 * i - 1
                                if kb >= 0:
                                    cand.setdefault(kb, []).append(((s_ % 2) * 4 + i, s_))
                    cx.pend = None

                    def mk_head(h):
                        def f():
                            ai = B.ring("accV", 2)
                            accv, avk = ((PA, "pa"), (PBK, "pbk"))[ai]
                            for ck in range(nkb // CH):
                                kvs = B.ring("kvC", 2)
                                B.dma("sp", KC[kvs][:], KD_T[h, :, ck * CH * 128:(ck + 1) * CH * 128], writes=[("kcC", kvs)], dsem=("kcC", kvs))
                                B.dma("sp", VC[kvs][:, :, 0:128], VD[h, :, ck * CH:(ck + 1) * CH, :], writes=[("vcC", kvs)], dsem=("vcC", kvs))
                                for tl in range(CH // 4):
                                    kb0 = ck * CH + tl * 4
                                    S, sk = MMR[B.ring("mmr", 4)]
                                    W_ = 4 * ncols
                                    B.mm(S[:, 0:W_], NEGI[:], MT[:, kb0 * ncols:(kb0 + 4) * ncols], True, False, ["NEGI", "MT"], [sk])
                                    for i in range(4):
                                        for (si, x_) in cand.get(kb0 + i, ()):
                                            rhs = HBP[:, h, x_, :] if inf["halo"] else BT[:, h, x_, :]
                                            B.mm(S[:, i * ncols:(i + 1) * ncols], SELH[:, si, :], rhs, False, False, ["SELH", "BT", "HBP"], [sk])
                                    for i in range(4):
                                        B.mm(S[:, i * ncols:(i + 1) * ncols], KC[kvs][:, (tl * 4 + i) * 128:(tl * 4 + i + 1) * 128], QT[qts][:, h, 0:ncols],
                                             False, i == 3, [("kcC", kvs), ("qtC", qts)], [sk])
                                    ps_ = B.ring("ptC", 3)
                                    B.act(PT[ps_][:, 0:W_], S[:, 0:W_], AF.Exp, [sk, "RBR"], [("ptC", ps_)], bias=RBR[:, 248 + h:249 + h])

                                    def pv(ps_=ps_, kb0=kb0, kvs=kvs, tl=tl, accv=accv, avk=avk):
                                        for i in range(4):
                                            kb = kb0 + i
                                            B.mm(accv[0:ncols, 0:129], PT[ps_][:, i * ncols:(i + 1) * ncols], VC[kvs][:, tl * 4 + i, 0:129], kb == 0, kb == nkb - 1,
                                                 [("ptC", ps_), ("vcC", kvs), ("vcC1", kvs)], [avk])
                                        if kb0 + 4 == nkb:
                                            oi = B.ring("osC", 2)
                                            B.recip(RC[oi][0:ncols, :], accv[0:ncols, 128:129], [avk], [("rcC", oi)])
                                            B.ts("dve", OS[oi][0:ncols, :], accv[0:ncols, 0:128], RC[oi][0:ncols, 0:1], None, ALU.mult, None, [avk, ("rcC", oi)], [("osC", oi)])
                                            B.tr(TPB[:, 1024 + oi * 128:1024 + oi * 128 + ncols], OS[oi][0:ncols, :], IDB[0:ncols, 0:ncols], [("osC", oi), "IDB"], [("tpb", 1)])
                                            B.cp("act", OT[ots][:, h, 0:ncols], TPB[:, 1024 + oi * 128:1024 + oi * 128 + ncols], [("tpb", 1)], [("otC", ots)])
                                    if cx.pend is not None:
                                        cx.pend()
                                    cx.pend = pv
                        return f
                    return [mk_head(h) for h in range(H)]

                def c4_finish(cx):
                    if cx.pend is not None:
                        cx.pend()
                        cx.pend = None
                    B.dma("pq", attd_v[:, :, cx.qb * 128:(cx.qb + 1) * 128], OT[cx.ots][:], reads=[("otC", cx.ots)], dsem=("otC", cx.ots))

                cur = c_load(qb_list[0])
                c1(cur)
                for f_ in c2_steps(cur):
                    f_()
                c3(cur)
                for idx in range(len(qb_list)):
                    nxt = None
                    steps = []
                    if idx + 1 < len(qb_list):
                        nxt = c_load(qb_list[idx + 1])
                        c1(nxt)
                        steps = c2_steps(nxt)
                    heads = c4_heads(cur)
                    k_ = 0
                    for hi, hf in enumerate(heads):
                        hf()
                        tgt = (len(steps) * (hi + 1)) // len(heads)
                        while k_ < tgt:
                            steps[k_]()
                            k_ += 1
                    c4_finish(cur)
                    if nxt is not None:
                        c3(nxt)
                    cur = nxt

            if stop_after >= "D":
              with contextlib.ExitStack() as eD:
                FM = B.sb(eD, "FM", [128, 4, 16 * 128], BF16)
                HFM = B.sb(eD, "HFM", [128, NKB * 16], BF16)
                SELIF = B.sb(eD, "SELIF", [128, 8, 128], F32)
                ONESW = B.sb(eD, "ONESW", [8, CH * 128], BF16)
                LK = [B.sb(eD, "LK%d" % i, [11, CH * 128], BF16) for i in range(2)]
                CQ11 = [B.sb(eD, "CQ11_%d" % i, [11, 128], BF16) for i in range(4)]
                CQH11 = B.sb(eD, "CQH11", [11, 16], BF16)
                QF = [B.sb(eD, "QF%d" % i, [128, 8, 512], BF16) for i in range(2)]
                KC = [B.sb(eD, "KCd%d" % i, [128, CH * 128], BF16) for i in range(2)]
                VC = [B.sb(eD, "VCd%d" % i, [128, CH, 130], BF16) for i in range(2)]
                PT = [B.sb(eD, "PTd%d" % i, [128, 512], BF16) for i in range(3)]
                OS = [B.sb(eD, "OSd%d" % i, [128, 128], BF16) for i in range(2)]
                RC = [B.sb(eD, "RCd%d" % i, [128, 1], F32) for i in range(2)]
                OT = [B.sb(eD, "OTd%d" % i, [128, 8, 512], BF16) for i in range(2)]
                for i in range(2):
                    B.memset("pool", VC[i][:, :, 128:129], 1.0, [("vcD1", i)])
                for i in range(8):
                    B.ts("dve", SELIF[:, i, :], IDF[:], SELO[:, i:i + 1], None, ALU.mult, None, ["IDF", "SELO"], ["SELIF"])
                for kb in range(NKB):
                    B.ts("dve", HFM[:, kb * 16:(kb + 1) * 16], HPOSR[:], IOKP[:, 0:1], float(128 * kb), ALU.add, ALU.is_lt, ["HPOSR", "IOKP"], ["HFM"])
                B.memset("pool", ONESW[:], 1.0, ["ONESW"])
                for c_ in range(4):
                    B.memset("pool", CQ11[c_][:], 1.0, [("cq11", c_)])
                B.memset("pool", CQH11[:], 1.0, ["cqh"])
                with contextlib.ExitStack() as e1:
                    PH = B.sb(e1, "PH", [128, 1024], BF16); PM = B.sb(e1, "PM", [128, 1024], BF16); PL = B.sb(e1, "PL", [128, 1024], BF16)
                    R1 = B.sb(e1, "R1", [128, 1024], F32); R2 = B.sb(e1, "R2", [128, 1024], F32)
                    STG = [B.sb(e1, "STGk%d" % i, [128, 128], BF16) for i in range(4)]
                    cum2 = CUML[:].rearrange("p a b -> p (a b)")
                    B.cp("dve", PH[:], cum2, ["CUML"], ["PH"])
                    B.tt("dve", R1[:], cum2, PH[:], ALU.subtract, ["CUML", "PH"], ["R1"])
                    B.cp("dve", PM[:], R1[:], ["R1"], ["PM"])
                    B.tt("dve", R2[:], R1[:], PM[:], ALU.subtract, ["R1", "PM"], ["R2"])
                    B.cp("dve", PL[:], R2[:], ["R2"], ["PL"])
                    for h in range(H):
                        for r_, (pc, pck) in enumerate(((PH, "PH"), (PM, "PM"), (PL, "PL"))):
                            ti = B.ring("ckT", 8)
                            hk = ("tpb", 0)
                            src = pc[:].rearrange("p (a b) -> p a b", b=8)[:, :, h]
                            B.tr(TPB[:, ti * 128:(ti + 1) * 128], src, IDB[:], [pck, "IDB"], [hk])
                            si = B.ring("ckS", 4)
                            B.cp("dve" if r_ % 2 == 0 else "act", STG[si][:], TPB[:, ti * 128:(ti + 1) * 128], [hk], [("ckS", si)])
                            B.dma("pq", CKR[h, r_, :, :], STG[si][:], reads=[("ckS", si)], dsem=("ckS", si))
                P.consts.update(["SELIF", "ONESW", "HFM"])
                qf_v = QF_T.rearrange("h p t -> p h t"); attf_v = ATTF_T.rearrange("h p t -> p h t")
                ACCS = [(PP[1][:, 0, 0:129], "pp1a"), (PP[1][:, 1, 0:129], "pp1b"), (PA[:, 0:129], "pa"), (PBK[:, 0:129], "pbk")]
                slots = list(range(nslot_d)) + [NSLOT]
                for sl_ in slots:
                    halo = sl_ == NSLOT
                    if not halo:
                        par = sl_ % 2; vs = 16 * sl_; nkb = vs + 16; ncols = 128; nq = 4
                    else:
                        par = None; vs = 0; nkb = NKB; ncols = 16; nq = 1
                    qfs = B.ring("qfD", 2)
                    ntok = 512 if not halo else 128
                    B.dma("sp", QF[qfs][:, :, 0:ntok], qf_v[:, :, sl_ * 512:sl_ * 512 + ntok], writes=[("qfD", qfs)], dsem=("qfD", qfs))
                    if not halo:
                        for c in range(4):
                            for r in range(16):
                                B.ts("dve", FM[:, c, r * 128:(r + 1) * 128], IOKP[:, 0:128], CVAL[:, par * 4 + c:par * 4 + c + 1], float(128 * r),
                                     ALU.add, ALU.is_lt, ["IOKP", "CVAL"], ["FM"])
                            for i in range(4):
                                B.mm(PP[0][0:8, 0, 0:128], CUML[:, vs + 4 * i + c, :], SELIF[:, par * 4 + i, :], i == 0, i == 3, ["CUML", "SELIF"], ["pp0a"])
                            B.ts("dve", CQ11[c][0:8, :], PP[0][0:8, 0, 0:128], -1.0, None, ALU.mult, None, ["pp0a"], [("cq11", c)])
                    else:
                        first = True
                        lst = [(s_, i) for s_ in range(NSLOT) for i in range(4) if 16 * s_ + 4 * i - 1 >= 0]
                        for n_, (s_, i) in enumerate(lst):
                            kb = 16 * s_ + 4 * i - 1
                            B.mm(PP[0][0:8, 0, 0:16], CUML[:, kb, :], HSELT[:, (s_ * 4 + i) * 16:(s_ * 4 + i + 1) * 16], n_ == 0, n_ == len(lst) - 1,
                                 ["CUML", "HSELT"], ["pp0a"])
                        B.ts("dve", CQH11[0:8, :], PP[0][0:8, 0, 0:16], -1.0, None, ALU.mult, None, ["pp0a"], ["cqh"])
                    ots = B.ring("otD", 2)
                    if halo:
                        B.memset("pool", OT[ots][:, :, 0:128], 0.0, [("otD", ots)])
                    pend = None
                    for h in range(H):
                        for ck in range(nkb // CH):
                            kvs = B.ring("kvD", 2)
                            B.dma("sp", KC[kvs][:], KF_T[h, :, ck * CH * 128:(ck + 1) * CH * 128], writes=[("kcD", kvs)], dsem=("kcD", kvs))
                            B.dma("sp", VC[kvs][:, :, 0:128], VF[h, :, ck * CH:(ck + 1) * CH, :], writes=[("vcD", kvs)], dsem=("vcD", kvs))
                            B.ts("pool", LK[kvs][0:8, :], ONESW[:], IDF[0:8, h:h + 1], None, ALU.mult, None, ["ONESW", "IDF"], [("lkD", kvs)])
                            B.dma("sp", LK[kvs][8:11, :], CKR[h, :, ck * CH:(ck + 1) * CH, :].rearrange("r k p -> r (k p)"), writes=[("lkD", kvs)], dsem=("lkD", kvs))
                            for c in range(nq):
                                accv, avk = ACCS[c]
                                for tl in range(CH // 4):
                                    kb0 = ck * CH + tl * 4
                                    S, sk = MMR[B.ring("mmrD", 2)]
                                    W_ = 4 * ncols
                                    if not halo:
                                        rq, rqk = CQ11[c][0:11, :], ("cq11", c)
                                        qap = QF[qfs][:, h, c * 128:(c + 1) * 128]
                                    else:
                                        rq, rqk = CQH11[0:11, :], "cqh"
                                        qap = QF[qfs][:, h, 0:16]
                                    for i in range(4):
                                        B.mm(S[:, i * ncols:(i + 1) * ncols], LK[kvs][0:11, (tl * 4 + i) * 128:(tl * 4 + i + 1) * 128], rq, i == 0, False,
                                             [("lkD", kvs), rqk], [sk])
                                    if not halo:
                                        if kb0 >= vs:
                                            B.mm(S[:, 0:W_], NEGI[:], FM[:, c, (kb0 - vs) * 128:(kb0 - vs + 4) * 128], False, False, ["NEGI", "FM"], [sk])
                                    else:
                                        B.mm(S[:, 0:W_], NEGI[:], HFM[:, kb0 * 16:(kb0 + 4) * 16], False, False, ["NEGI", "HFM"], [sk])
                                    for i in range(4):
                                        B.mm(S[:, i * ncols:(i + 1) * ncols], KC[kvs][:, (tl * 4 + i) * 128:(tl * 4 + i + 1) * 128], qap,
                                             False, i == 3, [("kcD", kvs), ("qfD", qfs)], [sk])
                                    ps_ = B.ring("ptD", 3)
                                    B.act(PT[ps_][:, 0:W_], S[:, 0:W_], AF.Exp, [sk], [("ptD", ps_)])

                                    def pv(ps_=ps_, kb0=kb0, kvs=kvs, tl=tl, accv=accv, avk=avk, h=h, c=c):
                                        for i in range(4):
                                            kb = kb0 + i
                                            B.mm(accv[0:ncols, :], PT[ps_][:, i * ncols:(i + 1) * ncols], VC[kvs][:, tl * 4 + i, 0:129], kb == 0, kb == nkb - 1,
                                                 [("ptD", ps_), ("vcD", kvs), ("vcD1", kvs)], [avk])
                                        if kb0 + 4 == nkb:
                                            oi = B.ring("osD", 2)
                                            B.recip(RC[oi][0:ncols, :], accv[0:ncols, 128:129], [avk], [("rcD", oi)])
                                            B.ts("dve", OS[oi][0:ncols, :], accv[0:ncols, 0:128], RC[oi][0:ncols, 0:1], None, ALU.mult, None, [avk, ("rcD", oi)], [("osD", oi)])
                                            B.tr(TPB[:, oi * 128:oi * 128 + ncols], OS[oi][0:ncols, :], IDB[0:ncols, 0:ncols], [("osD", oi), "IDB"], [("tpb", 0)])
                                            B.cp("dve", OT[ots][:, h, c * 128:c * 128 + ncols], TPB[:, oi * 128:oi * 128 + ncols], [("tpb", 0)], [("otD", ots)])
                                    if pend is not None:
                                        pend()
                                    pend = pv
                    pend()
                    B.dma("pq", attf_v[:, :, sl_ * 512:sl_ * 512 + ntok], OT[ots][:, :, 0:ntok], reads=[("otD", ots)], dsem=("otD", ots))
                    if dbg and sl_ in (0, NSLOT):
                        d_ = dbg_t("otf%d" % sl_, [128, 8 * 512], BF16)
                        B.dma("sp", d_[:, :], OT[ots][:].rearrange("p a b -> p (a b)"), reads=[("otD", ots)], dsem="dbg")

        if stop_after >= "F":
          with contextlib.ExitStack() as eE:
            CONVW = B.sb(eE, "CONVW", [128, 3, 2 * FC], F32); CONVB = B.sb(eE, "CONVB", [128, 2 * FC], F32)
            UH = B.sb(eE, "UH", [128, 2 * FC, 16], F32)
            XT = [B.sb(eE, "XTe%d" % i, [128, D], F32) for i in range(4)]
            H2T = B.sb(eE, "H2T", [128, DC, 512], BF16)
            GC = [B.sb(eE, "GC%d" % i, [128, 512], F32) for i in range(2)]
            B.dma("sp", CONVW[:], convw[:, :, :], writes=["CONVW"], dsem="c_CONVW")
            B.dma("sp", CONVB[:], convb[:, :], writes=["CONVB"], dsem="c_CONVB")
            P.consts.update(["CONVW", "CONVB"])
            af_v = ATTF_T.rearrange("h p t -> p h t"); ad_v = ATTD_T.rearrange("h p t -> p h t")
            wo_v = WO_S.rearrange("(k p) n -> p k n", p=128); wfo_v = WFO_S.rearrange("(k p) n -> p k n", p=128)

            def phase_E(gi):
                ntile = 4 if gi < NSLOT else 1
                ncols = ntile * 128
                tok0 = gi * 512
                with contextlib.ExitStack() as e1:
                    HT = B.sb(e1, "HTe", [128, DC, 512], BF16)
                    AFT = B.sb(e1, "AFT", [128, 8, 512], BF16); ADT = B.sb(e1, "ADT", [128, 8, 512], BF16)
                    XN = [B.sb(e1, "XNe%d" % i, [128, D], BF16) for i in range(2)]
                    WS = [B.sb(e1, "WSe%d" % i, [128, 48, 128], BF16) for i in range(3)]
                    MGT = B.sb(e1, "MGT", [128, DC, 512], BF16)
                    SG = [B.sb(e1, "SGe%d" % i, [128, 512], F32) for i in range(4)]
                    WOC = [B.sb(e1, "WOC%d" % i, [128, DC, 256], BF16) for i in range(2)]
                    TM = [B.sb(e1, "TMe%d" % i, [128, 256], F32) for i in range(2)]
                    B.dma("sp", AFT[:, :, 0:ncols], af_v[:, :, tok0:tok0 + ncols], writes=["AFT"], dsem="AFT")
                    B.dma("sp", ADT[:, :, 0:ncols], ad_v[:, :, tok0:tok0 + ncols], writes=["ADT"], dsem="ADT")
                    for t in range(ntile):
                        B.dma("sp", XT[t][:], xo[tok0 + t * 128:tok0 + (t + 1) * 128, :], writes=[("xtE", t)], dsem=("xtE", t))
                        ns = B.ring("xnE", 2)
                        norm_tile(XT[t][:], ("xtE", t), 0, HT, "HTe", t * 128, XN[ns], ("xnE", ns), B.ring("stat", 8))
                    for dcx in range(DC):
                        ws = B.ring("wsE", 3); wk = ("wsE", ws)
                        B.dma("sp", WS[ws][:, 0:8, :], WOF_S[dcx, :, :, :], writes=[wk], dsem=wk)
                        B.dma("sp", WS[ws][:, 8:16, :], WOD_S[dcx, :, :, :], writes=[wk], dsem=wk)
                        B.dma("sp", WS[ws][:, 16:32, :], WG_S[dcx, :, :, :], writes=[wk], dsem=wk)
                        B.dma("sp", WS[ws][:, 32:48, :], WG_S[16 + dcx, :, :, :], writes=[wk], dsem=wk)
                        (pyf, kyf), (pyd, kyd), (pga, kga), (pgb, kgb) = MMR
                        for h in range(H):
                            B.mm(pyf[:, 0:ncols], WS[ws][:, h, :], AFT[:, h, 0:ncols], h == 0, h == H - 1, [wk, "AFT"], [kyf])
                        for h in range(H):
                            B.mm(pyd[:, 0:ncols], WS[ws][:, 8 + h, :], ADT[:, h, 0:ncols], h == 0, h == H - 1, [wk, "ADT"], [kyd])
                        for dc in range(DC):
                            B.mm(pga[:, 0:ncols], WS[ws][:, 16 + dc, :], HT[:, dc, 0:ncols], dc == 0, dc == DC - 1, [wk, "HTe"], [kga])
                        for dc in range(DC):
                            B.mm(pgb[:, 0:ncols], WS[ws][:, 32 + dc, :], HT[:, dc, 0:ncols], dc == 0, dc == DC - 1, [wk, "HTe"], [kgb])
                        B.act(SG[0][:, 0:ncols], pga[:, 0:ncols], AF.Sigmoid, [kga], [("sgE", 0)])
                        B.act(SG[1][:, 0:ncols], pgb[:, 0:ncols], AF.Sigmoid, [kgb], [("sgE", 1)])
                        B.tt("dve", SG[2][:, 0:ncols], pyf[:, 0:ncols], SG[0][:, 0:ncols], ALU.mult, [kyf, ("sgE", 0)], [("sgE", 2)])
                        B.tt("dve", SG[3][:, 0:ncols], pyd[:, 0:ncols], SG[1][:, 0:ncols], ALU.mult, [kyd, ("sgE", 1)], [("sgE", 3)])
                        B.tt("pool", MGT[:, dcx, 0:ncols], SG[2][:, 0:ncols], SG[3][:, 0:ncols], ALU.add, [("sgE", 2), ("sgE", 3)], ["MGT"])
                    for cc in range(8):
                        wc = B.ring("wocE", 2); wck = ("wocE", wc)
                        B.dma("sp", WOC[wc][:], wo_v[:, :, cc * 256:(cc + 1) * 256], writes=[wck], dsem=wck)
                        gs_ = B.ring("gcE", 2); gk = ("gcE", gs_)
                        B.dma("sp", GC[gs_][:, 0:256], GROW_S[:, cc * 256:(cc + 1) * 256], writes=[gk], dsem=gk)
                        for t in range(ntile):
                            pst, pk = MMR[B.ring("mmr", 4)]
                            for dc in range(DC):
                                B.mm(pst[:, 0:256], MGT[:, dc, t * 128:(t + 1) * 128], WOC[wc][:, dc, :], dc == 0, dc == DC - 1, ["MGT", wck], [pk])
                            ti = B.ring("tmE", 2)
                            B.tt("dve", TM[ti][:], pst[:, 0:256], GC[gs_][:, 0:256], ALU.mult, [pk, gk], [("tmE", ti)])
                            B.tt("pool", XT[t][:, cc * 256:(cc + 1) * 256], XT[t][:, cc * 256:(cc + 1) * 256], TM[ti][:], ALU.add, [("tmE", ti), ("xtE", t)], [("xtE", t)])
                    for t in range(ntile):
                        ns = B.ring("xnE", 2)
                        norm_tile(XT[t][:], ("xtE", t), 1, H2T, "H2T", t * 128, XN[ns], ("xnE", ns), B.ring("stat", 8))

            def ffn_in_chunk(f, WFI, ncols, pst, pk):
                ws = B.ring("wfiF", 3); wk = ("wfiF", ws)
                B.dma("sp", WFI[ws][:], WFI_S[f, :, :, :], writes=[wk], dsem=wk)
                for dc in range(DC):
                    B.mm(pst[:, 0:ncols], WFI[ws][:, dc, :], H2T[:, dc, 0:ncols], dc == 0, dc == DC - 1, [wk, "H2T"], [pk])

            phase_E(NSLOT)
            with contextlib.ExitStack() as e1:
                WFI = [B.sb(e1, "WFIh%d" % i, [128, DC, 128], BF16) for i in range(3)]
                for f in range(2 * FC):
                    pst, pk = MMR[B.ring("mmr", 4)]
                    ffn_in_chunk(f, WFI, 16, pst, pk)
                    B.tt("dve", UH[:, f, :], pst[:, 0:16], HVAL[:], ALU.mult, [pk, "HVAL"], ["UH"])
            P.consts.add("UH")
            for s_ in range(nslot_f):
                phase_E(s_)
                with contextlib.ExitStack() as e1:
                    WFI = [B.sb(e1, "WFIf%d" % i, [128, DC, 128], BF16) for i in range(3)]
                    GT = B.sb(e1, "GTf", [128, FC, 512], BF16)
                    UB = [B.sb(e1, "UBf%d" % i, [128, 516], F32) for i in range(4)]
                    CAB = [B.sb(e1, "CABf%d" % i, [128, 512], F32) for i in range(4)]
                    SA = [B.sb(e1, "SAf%d" % i, [128, 512], F32) for i in range(2)]
                    WFO = [B.sb(e1, "WFOf%d" % i, [128, 11, 512], BF16) for i in range(3)]
                    OST = [B.sb(e1, "OSTf%d" % i, [128, 512], F32) for i in range(3)]
                    TMF = [B.sb(e1, "TMf%d" % i, [128, 512], F32) for i in range(2)]
                    for fp in range(FC):
                        cabs = []
                        for f in (fp, fp + FC):
                            pst, pk = MMR[B.ring("mmr", 4)]
                            ffn_in_chunk(f, WFI, 512, pst, pk)
                            ui = B.ring("ubF", 4); uk = ("ubF", ui)
                            B.cp("act", UB[ui][:, 2:514], pst[:, :], [pk], [uk])
                            B.cp("pool", UB[ui][:, 0:2], UH[:, f, 2 * s_:2 * s_ + 2], ["UH"], [uk])
                            ci = B.ring("cabF", 4); ck_ = ("cabF", ci)
                            B.ts("dve", CAB[ci][:], UB[ui][:, 2:514], CONVW[:, 2, f:f + 1], CONVB[:, f:f + 1], ALU.mult, ALU.add, [uk, "CONVW", "CONVB"], [ck_])
                            B.stt(CAB[ci][:], UB[ui][:, 1:513], CONVW[:, 1, f:f + 1], CAB[ci][:], ALU.mult, ALU.add, [uk, ck_, "CONVW"], [ck_])
                            B.stt(CAB[ci][:], UB[ui][:, 0:512], CONVW[:, 0, f:f + 1], CAB[ci][:], ALU.mult, ALU.add, [uk, ck_, "CONVW"], [ck_])
                            cabs.append((ci, ck_))
                        si = B.ring("saF", 2); sk_ = ("saF", si)
                        B.act(SA[si][:], CAB[cabs[0][0]][:], AF.Silu, [cabs[0][1]], [sk_])
                        B.tt("pool", GT[:, fp, :], SA[si][:], CAB[cabs[1][0]][:], ALU.mult, [sk_, cabs[1][1]], ["GTf"])
                    for cc in range(4):
                        gs_ = B.ring("gcE", 2); gk = ("gcE", gs_)
                        B.dma("sp", GC[gs_][:], GROW_S[:, D + cc * 512:D + (cc + 1) * 512], writes=[gk], dsem=gk)
                        for hf in range(4):
                            wo_ = B.ring("wfoF", 3); wok = ("wfoF", wo_)
                            B.dma("sp", WFO[wo_][:], wfo_v[:, hf * 11:(hf + 1) * 11, cc * 512:(cc + 1) * 512], writes=[wok], dsem=wok)
                            for t in range(4):
                                pst, pk = MMR[t]
                                for ki in range(11):
                                    kc_ = hf * 11 + ki
                                    B.mm(pst[:, :], GT[:, kc_, t * 128:(t + 1) * 128], WFO[wo_][:, ki, :], kc_ == 0, kc_ == FC - 1, ["GTf", wok], [pk])
                        for t in range(4):
                            pst, pk = MMR[t]
                            ti = B.ring("tmF", 2)
                            B.tt("dve", TMF[ti][:], pst[:, :], GC[gs_][:], ALU.mult, [pk, gk], [("tmF", ti)])
                            oi = B.ring("ostF", 3); ok_ = ("ostF", oi)
                            B.tt("pool", OST[oi][:], TMF[ti][:], XT[t][:, cc * 512:(cc + 1) * 512], ALU.add, [("tmF", ti), ("xtE", t)], [ok_])
                            B.dma("pq", out[s_ * 512 + t * 128:s_ * 512 + (t + 1) * 128, cc * 512:(cc + 1) * 512], OST[oi][:], reads=[ok_], dsem=ok_)
        P.barrier_dma("sp")
        P.emit()
        if os.environ.get("MK_VERBOSE"):
            print("n_ins", P.n_ins, {k: len(v) for k, v in P.streams.items()}, "nsem", len(P.phys_total), "arena_peak", B.arena_peak, flush=True)
    return dbg_out


def host_inputs(inp, core):
    b, j = core // 4, core % 4
    f32 = np.float32
    x = inp["x"]
    groups = _own_groups(j)
    o_par = (j, 3 - j)
    xo = np.zeros((NTOK, D), f32)
    for s, g in enumerate(groups):
        xo[s * 512:(s + 1) * 512] = x[b, g * 512:(g + 1) * 512]
        if g > 0:
            xo[NSLOT * 512 + 2 * s:NSLOT * 512 + 2 * s + 2] = x[b, g * 512 - 2:g * 512]
    col = lambda v: np.ascontiguousarray(np.asarray(v, f32).reshape(-1, 128).T)
    gains = np.stack([inp["q_norm_fox"][0] * SCALE, inp["k_norm_fox"][0], inp["q_norm_dsa"][0] * SCALE, inp["k_norm_dsa"][0],
                      inp["kv_norm_g"][0][:128], inp["kv_norm_g"][0][128:]], axis=1).astype(f32)
    cval = np.zeros((128, 8), f32); selo = np.zeros((128, 8), f32)
    for par in range(2):
        for c in range(4):
            cval[:, par * 4 + c] = 512 * o_par[par] + 128 * c
            selo[:, par * 4 + c] = 1.0 if o_par[par] == c else 0.0
    pos = np.zeros(128, f32); hval = np.zeros((128, 16), f32)
    hselt = np.zeros((128, NSLOT, 4, 16), f32)
    for s, g in enumerate(groups):
        for e in range(2):
            if g > 0:
                pos[2 * s + e] = 512 * g - 2 + e
                hval[:, 2 * s + e] = 1.0
                hselt[126 + e, s, o_par[s % 2], 2 * s + e] = 1.0
    hp2 = (np.arange(128, dtype=f32) - pos).reshape(128, 1)
    hposr = np.broadcast_to(pos[None, :16], (128, 16)).astype(f32)
    convw = inp["conv_w"][0].reshape(3, 2 * FC, 128).transpose(2, 0, 1)
    return {
        "xb": np.ascontiguousarray(x[b]), "xo": xo, "cT": col(inp["c"][b]),
        "w_ada": inp["w_ada"][0], "b_ada_c": col(inp["b_ada"][0]), "b_ada_r": inp["b_ada"][0].reshape(1, -1),
        "n1c": col(inp["norm1_g"][0]), "n2c": col(inp["norm2_g"][0]), "w_in": inp["w_in"][0],
        "b_forget": inp["b_forget"][0].reshape(1, 8), "gains": np.ascontiguousarray(gains), "w_ukv": inp["w_ukv"][0],
        "w_out_fox": inp["w_out_fox"][0], "w_out_dsa": inp["w_out_dsa"][0], "w_out": inp["w_out"][0],
        "w_ffn_in": inp["w_ffn_in"][0], "convw": np.ascontiguousarray(convw, dtype=f32), "convb": col(inp["conv_b"][0]),
        "w_ffn_out": inp["w_ffn_out"][0], "relb": inp["rel_bias"].reshape(1, 256),
        "cval": cval, "selo": selo, "selv": np.zeros((128, 64), f32), "hp2": hp2, "hposr": hposr,
        "hselt": np.ascontiguousarray(hselt.reshape(128, 512)), "hval": hval,
    }


def kernel(**inputs):
    inp = {k: np.asarray(v) for k, v in inputs.items()}
    nc = bass.Bass("TRN2", target_bir_lowering=False)
    build_program(nc)
    in_maps = [host_inputs(inp, c) for c in range(8)]
    res = run_bass_kernel_spmd(nc, in_maps, core_ids=list(range(8)))
    out = np.zeros((2, L, D), np.float32)
    for c in range(8):
        b, j = c // 4, c % 4
        o = np.asarray(res.results[c]["out"])
        for s, g in enumerate(_own_groups(j)):
            out[b, g * 512:(g + 1) * 512] = o[s * 512:(s + 1) * 512]
    return out
```

```python
import contextlib
import math
import os
import numpy as np
import ml_dtypes
import concourse.bass as bass
import concourse.mybir as mybir
from concourse.bass_utils import run_bass_kernel_spmd

F32 = mybir.dt.float32
BF16 = mybir.dt.bfloat16
I32 = mybir.dt.int32
AF = mybir.ActivationFunctionType
ALU = mybir.AluOpType
AX = mybir.AxisListType

D = 2048
DC = 16
L = 16384
NKB = 128
NSLOT = 8
NT = 33
NTOK = NT * 128
H = 8
DH = 128
KVL = 256
NIDX = 16
IDXD = 64
DFF = 5632
FC = 44
TOPK = 256
EPS = 1e-6
NEG = -30000.0
NBIS = 14
SCALE = DH ** -0.5
WSCALE = (IDXD ** -0.5) * (NIDX ** -0.5)

C_QF, C_KF, C_VF, C_FG, C_QD, C_CKV, C_QI, C_KI, C_WI, C_GA, C_GB = (
    0, 1024, 2048, 3072, 3080, 4104, 4360, 5384, 5448, 5464, 7512)


class _Op:
    __slots__ = ("eng", "st", "fn", "dsem", "waits_d", "waits_e", "inc", "cnt", "is_dma", "order")

    def __init__(self, eng, st, fn, dsem):
        self.eng = eng
        self.st = st
        self.fn = fn
        self.dsem = dsem
        self.is_dma = dsem is not None
        self.waits_d = {}
        self.waits_e = {}
        self.inc = False
        self.cnt = 0


class Prog:
    def __init__(self, nc):
        self.nc = nc
        self.streams = {"pe": [], "act": [], "dve": [], "pool": [], "sp": []}
        self.last_w = {}
        self.readers = {}
        self.key2phys = {}
        self.phys_total = []
        self.free_phys = []
        self.consts = set()

    def _dep(self, o, d, kind):
        if d is o:
            return
        if d.is_dma:
            if o.waits_d.get(d.dsem, 0) < d.cnt:
                o.waits_d[d.dsem] = d.cnt
            return
        if d.st == o.st and not o.is_dma:
            if o.st == "pe":
                return
            if kind == "war":
                return
        d.inc = True
        prev = o.waits_e.get(d.st)
        if prev is None or prev.order < d.order:
            o.waits_e[d.st] = d

    def op(self, eng, fn, reads=(), writes=(), dsem=None):
        st = "pool" if eng == "pq" else eng
        o = _Op(eng, st, fn, dsem)
        lst = self.streams[st]
        o.order = len(lst)
        for k in reads:
            w = self.last_w.get(k)
            if w is not None:
                self._dep(o, w, "raw")
        for k in writes:
            w = self.last_w.get(k)
            if w is not None:
                self._dep(o, w, "waw")
            for r in self.readers.get(k, ()):
                self._dep(o, r, "war")
        if o.is_dma:
            ph = self.key2phys.get(dsem)
            if ph is None:
                if self.free_phys:
                    ph = self.free_phys.pop()
                else:
                    ph = len(self.phys_total)
                    self.phys_total.append(0)
                self.key2phys[dsem] = ph
            self.phys_total[ph] += 1
            o.dsem = ph
            o.cnt = self.phys_total[ph]
        for k in reads:
            if k in self.consts:
                continue
            self.readers.setdefault(k, []).append(o)
        for k in writes:
            self.last_w[k] = o
            self.readers[k] = []
        lst.append(o)
        return o

    def barrier_dma(self, eng, dsems=None):
        st = "pool" if eng == "pq" else eng
        o = _Op(eng, st, None, None)
        o.order = len(self.streams[st])
        for ph, c in enumerate(self.phys_total):
            if c:
                o.waits_d[ph] = c
        self.streams[st].append(o)
        return o

    def full_barrier(self):
        lasts = {}
        for st in ("pe", "act", "dve", "pool"):
            for o in reversed(self.streams[st]):
                if (not o.is_dma) and o.fn is not None:
                    lasts[st] = o
                    break
        for st in self.streams:
            b = _Op(st, st, None, None)
            b.order = len(self.streams[st])
            for st2, d in lasts.items():
                if st2 != st:
                    d.inc = True
                    b.waits_e[st2] = d
            for ph, c in enumerate(self.phys_total):
                if c:
                    b.waits_d[ph] = c
            self.streams[st].append(b)
        self.free_phys = [ph for ph in range(len(self.phys_total)) if ph not in self.free_phys] + self.free_phys
        self.key2phys = {}

    def emit(self):
        nc = self.nc
        for st, ops in self.streams.items():
            c = 0
            for o in ops:
                if (not o.is_dma) and o.inc:
                    c += 1
                    o.cnt = c
        dsem_keys = list(range(len(self.phys_total)))
        self.n_ins = 0
        with contextlib.ExitStack() as es:
            esem = {st: es.enter_context(nc.semaphore("e_" + st)) for st in ("pe", "act", "dve", "pool")}
            dsem = {k: es.enter_context(nc.semaphore("d_%d" % i)) for i, k in enumerate(dsem_keys)}
            block = es.enter_context(nc.Block())

            def run(st, eng):
                waited = {}
                for o in self.streams[st]:
                    for k, v in o.waits_d.items():
                        key = ("d", k)
                        val = v * 16
                        if waited.get(key, 0) >= val:
                            continue
                        waited[key] = val
                        eng.wait_ge(dsem[k], val)
                    for k, d in o.waits_e.items():
                        key = ("e", k)
                        val = d.cnt
                        if waited.get(key, 0) >= val:
                            continue
                        waited[key] = val
                        eng.wait_ge(esem[k], val)
                    if o.fn is None:
                        continue
                    ins = o.fn(eng)
                    self.n_ins += 1
                    if o.is_dma:
                        ins.then_inc(dsem[o.dsem], 16)
                    elif o.inc:
                        ins.then_inc(esem[st], 1)

            @block.tensor
            def _(e):
                run("pe", e)

            @block.scalar
            def _(e):
                run("act", e)

            @block.vector
            def _(e):
                run("dve", e)

            @block.gpsimd
            def _(e):
                run("pool", e)

            @block.sync
            def _(e):
                run("sp", e)


class Builder:
    def __init__(self, nc, dbg=None, stop_after=None, ng_a=32):
        self.nc = nc
        self.P = Prog(nc)
        self.dbg = dbg or {}
        self.stop_after = stop_after
        self.ng_a = ng_a
        self.rr = {}

    def dram_in(self, name, shape, dt=F32):
        return self.nc.dram_tensor(name, list(shape), dt, kind="ExternalInput").ap()

    def dram_out(self, name, shape, dt=F32):
        return self.nc.dram_tensor(name, list(shape), dt, kind="ExternalOutput").ap()

    def dram_tmp(self, name, shape, dt):
        return self.nc.dram_tensor(name, list(shape), dt, kind="Internal").ap()

    def init_arena(self, es, nwords=53000):
        self.arena = es.enter_context(self.nc.sbuf_tensor("arena", [128, nwords], F32))
        self.arena_words = nwords
        self.arena_off = 0
        self.arena_peak = 0
        self._marks = {}

    def _release(self, mark):
        self.P.full_barrier()
        self.arena_off = mark

    def sb(self, es, name, shape, dt):
        if not hasattr(es, "_mk_mark"):
            mark = self.arena_off
            es._mk_mark = mark
            es.callback(self._release, mark)
        esz = {F32: 4, BF16: 2, I32: 4}[dt]
        n = 1
        for d_ in shape[1:]:
            n *= d_
        words = (n * esz + 3) // 4
        if self.arena_off + words > self.arena_words:
            raise RuntimeError("SBUF arena overflow at %s: need %d words at off %d" % (name, words, self.arena_off))
        v = self.arena[0:shape[0], self.arena_off:self.arena_off + words]
        self.arena_off += words
        self.arena_peak = max(self.arena_peak, self.arena_off)
        if dt != F32:
            v = v.bitcast(dt)
        v = v[:, 0:n]
        if len(shape) == 3:
            v = v.rearrange("p (a b) -> p a b", b=shape[2])
        elif len(shape) == 4:
            v = v.rearrange("p (a b c) -> p a b c", b=shape[2], c=shape[3])
        return v

    def ps(self, es, name, shape, dt=F32):
        return es.enter_context(self.nc.psum_tensor(name, list(shape), dt))

    def ring(self, name, n):
        i = self.rr.get(name, 0)
        self.rr[name] = i + 1
        return i % n

    def dma(self, q, out, in_, reads=(), writes=(), dsem=None):
        return self.P.op(q, lambda e: e.dma_start(out=out, in_=in_), reads=reads, writes=writes, dsem=dsem)

    def mm(self, out, lhsT, rhs, start, stop, reads, writes, tile_position=None):
        if tile_position is None:
            fn = lambda e: e.matmul(out, lhsT=lhsT, rhs=rhs, start=start, stop=stop)
        else:
            fn = lambda e: e.matmul(out, lhsT=lhsT, rhs=rhs, start=start, stop=stop, tile_position=tile_position)
        return self.P.op("pe", fn, reads=reads, writes=writes)

    def tr(self, out, in_, ident, reads, writes):
        return self.P.op("pe", lambda e: e.transpose(out=out, in_=in_, identity=ident), reads=reads, writes=writes)

    def act(self, out, in_, func, reads, writes, scale=None, bias=None, accum_out=None):
        kw = {}
        if scale is not None:
            kw["scale"] = scale
        if bias is not None:
            kw["bias"] = bias
        if accum_out is not None:
            kw["accum_out"] = accum_out
        return self.P.op("act", lambda e: e.activation(out=out, in_=in_, func=func, **kw), reads=reads, writes=writes)

    def ts(self, eng, out, in0, s1, s2, op0, op1, reads, writes, accum_out=None):
        kw = {}
        if op1 is not None:
            kw["op1"] = op1
        if accum_out is not None:
            kw["accum_out"] = accum_out
        return self.P.op(eng, lambda e: e.tensor_scalar(out=out, in0=in0, scalar1=s1, scalar2=s2, op0=op0, **kw),
                         reads=reads, writes=writes)

    def tt(self, eng, out, in0, in1, op, reads, writes):
        return self.P.op(eng, lambda e: e.tensor_tensor(out=out, in0=in0, in1=in1, op=op), reads=reads, writes=writes)

    def stt(self, out, in0, scalar, in1, op0, op1, reads, writes):
        return self.P.op("dve", lambda e: e.scalar_tensor_tensor(out=out, in0=in0, scalar=scalar, in1=in1, op0=op0, op1=op1),
                         reads=reads, writes=writes)

    def cp(self, eng, out, in_, reads, writes):
        if eng == "act":
            return self.act(out, in_, AF.Copy, reads, writes)
        return self.P.op(eng, lambda e: e.tensor_copy(out=out, in_=in_), reads=reads, writes=writes)

    def recip(self, out, in_, reads, writes):
        return self.P.op("dve", lambda e: e.reciprocal(out=out, in_=in_), reads=reads, writes=writes)

    def memset(self, eng, ap, val, writes):
        return self.P.op(eng, lambda e: e.memset(ap, val), writes=writes)


def _own_groups(j):
    return [4 * s + (j if s % 2 == 0 else 3 - j) for s in range(NSLOT)]


def build_program(nc, dbg=False, stop_after="Z", ng_a=32, nqb_c=NT, nslot_d=NSLOT, nslot_f=NSLOT):
    B = Builder(nc)
    P = B.P
    es = contextlib.ExitStack()
    with es:
        xb = B.dram_in("xb", [L, D]); xo = B.dram_in("xo", [NTOK, D])
        cT = B.dram_in("cT", [128, DC])
        w_ada = B.dram_in("w_ada", [D, 6 * D]); b_ada_c = B.dram_in("b_ada_c", [128, 96]); b_ada_r = B.dram_in("b_ada_r", [1, 6 * D])
        n1c = B.dram_in("n1c", [128, DC]); n2c = B.dram_in("n2c", [128, DC])
        w_in = B.dram_in("w_in", [D, 9560])
        b_forget = B.dram_in("b_forget", [1, 8])
        gains = B.dram_in("gains", [128, 6])
        w_ukv = B.dram_in("w_ukv", [KVL, 2 * H * DH])
        w_of = B.dram_in("w_out_fox", [H * DH, D]); w_od = B.dram_in("w_out_dsa", [H * DH, D]); w_o = B.dram_in("w_out", [D, D])
        w_fi = B.dram_in("w_ffn_in", [D, 2 * DFF]); convw = B.dram_in("convw", [128, 3, 2 * FC]); convb = B.dram_in("convb", [128, 2 * FC])
        w_fo = B.dram_in("w_ffn_out", [DFF, D])
        relb = B.dram_in("relb", [1, 256])
        cval = B.dram_in("cval", [128, 8]); selo = B.dram_in("selo", [128, 8]); selv = B.dram_in("selv", [128, 64])
        hp2 = B.dram_in("hp2", [128, 1]); hposr = B.dram_in("hposr", [128, 16]); hselt = B.dram_in("hselt", [128, 512])
        hval = B.dram_in("hval", [128, 16])
        out = B.dram_out("out", [NSLOT * 512, D])

        KF_T = B.dram_tmp("KF_T", [H, 128, L], BF16); KD_T = B.dram_tmp("KD_T", [H, 128, L], BF16)
        VF = B.dram_tmp("VF", [H, 128, NKB, 128], BF16); VD = B.dram_tmp("VD", [H, 128, NKB, 128], BF16)
        KI2_T = B.dram_tmp("KI2_T", [128, L], BF16)
        QF_T = B.dram_tmp("QF_T", [H, 128, NTOK], BF16); QD_T = B.dram_tmp("QD_T", [H, 128, NTOK], BF16)
        QI_T = B.dram_tmp("QI_T", [8, 128, NTOK], BF16); WI_S = B.dram_tmp("WI_S", [NTOK, 16], F32)
        ATTF_T = B.dram_tmp("ATTF_T", [H, 128, NTOK], BF16); ATTD_T = B.dram_tmp("ATTD_T", [H, 128, NTOK], BF16)
        WQ_S = B.dram_tmp("WQ_S", [D, 3088], BF16)
        WOF_S = B.dram_tmp("WOF_S", [16, 128, 8, 128], BF16); WOD_S = B.dram_tmp("WOD_S", [16, 128, 8, 128], BF16)
        WG_S = B.dram_tmp("WG_S", [32, 128, 16, 128], BF16)
        WO_S = B.dram_tmp("WO_S", [D, D], BF16)
        WFI_S = B.dram_tmp("WFI_S", [2 * FC, 128, 16, 128], BF16)
        WFO_S = B.dram_tmp("WFO_S", [DFF, D], BF16)
        GROW_S = B.dram_tmp("GROW_S", [128, 2 * D], F32)
        dbg_out = {}

        def dbg_t(name, shape, dt=F32):
            if dbg:
                dbg_out[name] = B.dram_out("dbg_" + name, shape, dt)
            return dbg_out.get(name)

        B.init_arena(es)
        PP = [B.ps(es, "pp%d" % i, [128, 2, 512]) for i in range(2)]
        PA = B.ps(es, "pa", [128, 512]); PBK = B.ps(es, "pbk", [128, 512])
        TPB = B.ps(es, "tpb", [128, 2048], BF16)
        MMR = [(PP[0][:, 0, :], "pp0a"), (PP[0][:, 1, :], "pp0b"), (PP[1][:, 0, :], "pp1a"), (PP[1][:, 1, :], "pp1b")]

        IDB = B.sb(es, "IDB", [128, 128], BF16); IDF = B.sb(es, "IDF", [128, 128], F32)
        NEGI = B.sb(es, "NEGI", [128, 128], BF16)
        ONESB = B.sb(es, "ONESB", [128, 128], BF16); ONESF = B.sb(es, "ONESF", [128, 128], F32)
        TRIF = B.sb(es, "TRIF", [128, 128], F32)
        MODC = B.sb(es, "MODC", [128, 96], F32)
        GS = B.sb(es, "GS", [128, 2, DC], F32)
        CUML = B.sb(es, "CUML", [128, NKB, 8], F32)
        GAINS = B.sb(es, "GAINS", [128, 6], F32)
        CVAL = B.sb(es, "CVAL", [128, 8], F32); SELO = B.sb(es, "SELO", [128, 8], F32); SELV = B.sb(es, "SELV", [128, 64], F32)
        HP2 = B.sb(es, "HP2", [128, 1], F32); HPOSR = B.sb(es, "HPOSR", [128, 16], F32)
        HSELT = B.sb(es, "HSELT", [128, 512], F32); HVAL = B.sb(es, "HVAL", [128, 16], F32)
        RBR = B.sb(es, "RBR", [128, 256], F32)
        STAT = B.sb(es, "STAT", [128, 8, 4], F32)
        JUNKB = B.sb(es, "JUNKB", [128, 2048], BF16)

        for (t, src, key) in ((GAINS, gains, "GAINS"), (CVAL, cval, "CVAL"), (SELO, selo, "SELO"), (SELV, selv, "SELV"),
                              (HP2, hp2, "HP2"), (HPOSR, hposr, "HPOSR"), (HSELT, hselt, "HSELT"), (HVAL, hval, "HVAL")):
            B.dma("sp", t[:], src[:, :], writes=[key], dsem="c_" + key)
        B.dma("sp", RBR[:], relb.partition_broadcast(128)[:, 0, :], writes=["RBR"], dsem="c_RBR")

        B.memset("pool", IDF[:], 1.0, ["IDF"])
        P.op("pool", lambda e: e.affine_select(out=IDF[:], in_=IDF[:], pattern=[[-1, 128]], compare_op=ALU.is_equal, fill=0.0,
                                               base=0, channel_multiplier=1), reads=["IDF"], writes=["IDF"])
        B.cp("dve", IDB[:], IDF[:], ["IDF"], ["IDB"])
        B.ts("dve", NEGI[:], IDF[:], NEG, None, ALU.mult, None, ["IDF"], ["NEGI"])
        B.memset("pool", ONESB[:], 1.0, ["ONESB"]); B.memset("pool", ONESF[:], 1.0, ["ONESF"])
        B.memset("pool", TRIF[:], 1.0, ["TRIF"])
        P.op("pool", lambda e: e.affine_select(out=TRIF[:], in_=TRIF[:], pattern=[[1, 128]], compare_op=ALU.is_ge, fill=0.0,
                                               base=0, channel_multiplier=-1), reads=["TRIF"], writes=["TRIF"])
        with contextlib.ExitStack() as e0:
            GROW = B.sb(e0, "GROW", [128, 2, D], F32)
            P.consts.update(["IDB", "IDF", "NEGI", "ONESB", "ONESF", "TRIF", "GAINS", "CVAL", "SELO", "SELV", "HP2",
                             "HPOSR", "HSELT", "HVAL", "RBR"])

            CT = B.sb(e0, "CT", [128, DC], F32); CACT = B.sb(e0, "CACT", [128, DC], F32)
            CREP = B.sb(e0, "CREP", [128, DC, 128], F32)
            BADAC = B.sb(e0, "BADAC", [128, 96], F32); BRG = B.sb(e0, "BRG", [128, 2, D], F32)
            N12 = B.sb(e0, "N12", [128, 2, DC], F32)
            WA32 = [B.sb(e0, "WA32_%d" % i, [128, DC, 512], F32) for i in range(2)]
            WAB = WA32
            B.dma("sp", CT[:], cT[:, :], writes=["CT"], dsem="c_CT")
            B.dma("sp", BADAC[:], b_ada_c[:, :], writes=["BADAC"], dsem="c_BADAC")
            B.dma("sp", N12[:, 0, :], n1c[:, :], writes=["N12"], dsem="c_N12")
            B.dma("sp", N12[:, 1, :], n2c[:, :], writes=["N12"], dsem="c_N12")
            B.dma("sp", BRG[:, 0, :], b_ada_r[:, 2 * D:3 * D].partition_broadcast(128)[:, 0, :], writes=["BRG"], dsem="c_BRG")
            B.dma("sp", BRG[:, 1, :], b_ada_r[:, 5 * D:6 * D].partition_broadcast(128)[:, 0, :], writes=["BRG"], dsem="c_BRG")
            B.act(CACT[:], CT[:], AF.Silu, ["CT"], ["CACT"])
            for dc in range(DC):
                B.ts("dve", CREP[:, dc, :], ONESF[:], CACT[:, dc:dc + 1], None, ALU.mult, None, ["ONESF", "CACT"], ["CREP"])
            w_ada_v = w_ada.rearrange("(dc p) n -> p dc n", p=128)
            for ci in range(24):
                sl = ci % 2
                B.dma("sp", WA32[sl][:], w_ada_v[:, :, ci * 512:(ci + 1) * 512], writes=[("wa32", sl)], dsem=("wa32", sl))
                sec = ci // 4
                if sec in (2, 5):
                    gi = 0 if sec == 2 else 1
                    pst, pk = MMR[ci % 4]
                    for dc in range(DC):
                        B.mm(pst, CREP[:, dc, :], WAB[sl][:, dc, :], dc == 0, dc == DC - 1, ["CREP", ("wa32", sl)], [pk])
                    c0 = (ci % 4) * 512
                    B.tt("dve", GROW[:, gi, c0:c0 + 512], pst, BRG[:, gi, c0:c0 + 512], ALU.add, [pk, "BRG"], ["GROW"])
                else:
                    for m4 in range(4):
                        m = ci * 4 + m4
                        for dc in range(DC):
                            B.mm(PA[:, m:m + 1], WAB[sl][:, dc, m4 * 128:(m4 + 1) * 128], CACT[:, dc:dc + 1], dc == 0, dc == DC - 1,
                                 ["CACT", ("wa32", sl)], ["pa"])
            B.tt("dve", MODC[:], PA[:, 0:96], BADAC[:], ALU.add, ["pa", "BADAC"], ["MODC"])
            for i, (sc0, sh0) in enumerate(((16, 0), (64, 48))):
                B.ts("dve", GS[:, i, :], MODC[:, sc0:sc0 + 16], 1.0, None, ALU.add, None, ["MODC"], ["GS"])
                B.tt("dve", GS[:, i, :], GS[:, i, :], N12[:, i, :], ALU.mult, ["GS", "N12"], ["GS"])
            B.dma("sp", GROW_S[:, :], GROW[:].rearrange("p a b -> p (a b)"), reads=["GROW"], dsem="grow_st")
            if dbg:
                d_mod = dbg_t("modc", [128, 96]); d_grow = dbg_t("grow", [128, 2 * D])
                B.dma("sp", d_mod[:, :], MODC[:], reads=["MODC"], dsem="dbg")
                B.dma("sp", d_grow[:, :], GROW[:].rearrange("p a b -> p (a b)"), reads=["GROW"], dsem="dbg")
        P.consts.update(["MODC", "GS"])
        SH = (MODC[:, 0:16], MODC[:, 48:64])

        def norm_tile(xt_ap, xkey, which, HT, htkey, tcol0, XN, xnkey, si):
            st = STAT[:, si, :]
            sk = ("stat", si)
            B.act(JUNKB[:], xt_ap, AF.Square, [xkey], [sk], accum_out=st[:, 0:1])
            B.act(st[:, 1:2], st[:, 0:1], AF.Sqrt, [sk], [sk], scale=1.0 / D, bias=EPS)
            B.recip(st[:, 2:3], st[:, 1:2], [sk], [sk])
            B.ts("dve", XN, xt_ap, st[:, 2:3], None, ALU.mult, None, [xkey, sk], [xnkey])
            for dc in range(DC):
                hk = ("tpb", dc // 8)
                B.tr(TPB[:, dc * 128:(dc + 1) * 128], XN[:, dc * 128:(dc + 1) * 128], IDB[:], [xnkey, "IDB"], [hk])
            for dc in range(DC):
                hk = ("tpb", dc // 8)
                o_ap = HT[:, dc, tcol0:tcol0 + 128]
                if dc % 2 == 0:
                    B.act(o_ap, TPB[:, dc * 128:(dc + 1) * 128], AF.Identity, [hk, "GS", "MODC"], [htkey],
                          scale=GS[:, which, dc:dc + 1], bias=SH[which][:, dc:dc + 1])
                else:
                    B.ts("dve", o_ap, TPB[:, dc * 128:(dc + 1) * 128], GS[:, which, dc:dc + 1], SH[which][:, dc:dc + 1],
                         ALU.mult, ALU.add, [hk, "GS", "MODC"], [htkey])

        def prep_nat(src, K, N, dst, e1, name):
            kc_all = K // 128
            W32 = [B.sb(e1, name + "32_%d" % i, [128, DC, 512], F32) for i in range(2)]
            W16 = [B.sb(e1, name + "16_%d" % i, [128, DC, 512], BF16) for i in range(2)]
            srcv = src.rearrange("(k p) n -> p k n", p=128)
            dstv = dst.rearrange("(k p) n -> p k n", p=128)
            i = 0
            for k0 in range(0, kc_all, DC):
                kcb = min(DC, kc_all - k0)
                for c0 in range(0, N, 512):
                    cw = min(512, N - c0)
                    sl = i % 2
                    B.dma("sp", W32[sl][:, 0:kcb, 0:cw], srcv[:, k0:k0 + kcb, c0:c0 + cw], writes=[(name + "32", sl)], dsem=(name + "32", sl))
                    B.cp("dve" if i % 2 == 0 else "pool", W16[sl][:, 0:kcb, 0:cw], W32[sl][:, 0:kcb, 0:cw], [(name + "32", sl)], [(name + "16", sl)])
                    B.dma("pq", dstv[:, k0:k0 + kcb, c0:c0 + cw], W16[sl][:, 0:kcb, 0:cw], reads=[(name + "16", sl)], dsem=(name + "st", sl))
                    i += 1
            return [(name + "st", 0), (name + "st", 1)]

        def prep_tiled(src, K, c_lo, c_hi, dst, j_off, e1, name):
            kcb = K // 128
            W32 = [B.sb(e1, name + "32_%d" % i, [128, DC, 512], F32) for i in range(2)]
            W16 = [B.sb(e1, name + "16_%d" % i, [128, DC, 512], BF16) for i in range(2)]
            srcv = src.rearrange("(k p) n -> p k n", p=128)
            i = 0
            for c0 in range(c_lo, c_hi, 512):
                cw = min(512, c_hi - c0)
                nj = cw // 128
                sl = i % 2
                B.dma("sp", W32[sl][:, 0:kcb, 0:cw], srcv[:, :, c0:c0 + cw], writes=[(name + "32", sl)], dsem=(name + "32", sl))
                B.cp("dve" if i % 2 == 0 else "pool", W16[sl][:, 0:kcb, 0:cw], W32[sl][:, 0:kcb, 0:cw], [(name + "32", sl)], [(name + "16", sl)])
                j0 = j_off + (c0 - c_lo) // 128
                B.dma("pq", dst[j0:j0 + nj, :, :, :].rearrange("j p k c -> p k j c"),
                      W16[sl][:, 0:kcb, 0:cw].rearrange("p k (j c) -> p k j c", c=128), reads=[(name + "16", sl)], dsem=(name + "st", sl))
                i += 1
            return [(name + "st", 0), (name + "st", 1)]

        st_sems = []
        with contextlib.ExitStack() as e1:
            W32 = [B.sb(e1, "wq32_%d" % i, [128, DC, 512], F32) for i in range(2)]
            W16 = [B.sb(e1, "wq16_%d" % i, [128, DC, 512], BF16) for i in range(2)]
            w_in_v = w_in.rearrange("(k p) n -> p k n", p=128)
            wq_v = WQ_S.rearrange("(k p) n -> p k n", p=128)
            i = 0
            for (slo, shi, dlo) in ((C_QF, C_QF + 1024, 0), (C_QD, C_QD + 1024, 1024), (C_QI, C_QI + 1024, 2048), (C_WI, C_WI + 16, 3072)):
                for c0 in range(slo, shi, 512):
                    cw = min(512, shi - c0)
                    sl = i % 2
                    B.dma("sp", W32[sl][:, :, 0:cw], w_in_v[:, :, c0:c0 + cw], writes=[("wq32", sl)], dsem=("wq32", sl))
                    B.cp("dve" if i % 2 == 0 else "pool", W16[sl][:, :, 0:cw], W32[sl][:, :, 0:cw], [("wq32", sl)], [("wq16", sl)])
                    d0 = dlo + (c0 - slo)
                    B.dma("pq", wq_v[:, :, d0:d0 + cw], W16[sl][:, :, 0:cw], reads=[("wq16", sl)], dsem=("wqst", sl))
                    i += 1
            st_sems += [("wqst", 0), ("wqst", 1)]

        def prep_chunk_list(W32, W16):
            chunks = []

            def add_tiled(src, K, c_lo, c_hi, dst, j_off):
                kcb = K // 128
                srcv = src.rearrange("(k p) n -> p k n", p=128)
                for c0 in range(c_lo, c_hi, 512):
                    cw = min(512, c_hi - c0)
                    nj = cw // 128
                    j0 = j_off + (c0 - c_lo) // 128

                    def f(c0=c0, cw=cw, nj=nj, j0=j0):
                        sl = B.ring("pw", 2)
                        B.dma("pq", W32[sl][:, 0:kcb, 0:cw], srcv[:, :, c0:c0 + cw], writes=[("pw32", sl)], dsem=("pw32", sl))
                        B.cp("dve" if sl == 0 else "pool", W16[sl][:, 0:kcb, 0:cw], W32[sl][:, 0:kcb, 0:cw], [("pw32", sl)], [("pw16", sl)])
                        B.dma("pq", dst[j0:j0 + nj, :, :, :].rearrange("j p k c -> p k j c"),
                              W16[sl][:, 0:kcb, 0:cw].rearrange("p k (j c) -> p k j c", c=128), reads=[("pw16", sl)], dsem=("pwst", sl))
                    chunks.append(f)

            def add_nat(src, K, N, dst):
                kc_all = K // 128
                srcv = src.rearrange("(k p) n -> p k n", p=128)
                dstv = dst.rearrange("(k p) n -> p k n", p=128)
                for k0 in range(0, kc_all, DC):
                    kcb = min(DC, kc_all - k0)
                    for c0 in range(0, N, 512):
                        cw = min(512, N - c0)

                        def f(k0=k0, kcb=kcb, c0=c0, cw=cw):
                            sl = B.ring("pw", 2)
                            B.dma("pq", W32[sl][:, 0:kcb, 0:cw], srcv[:, k0:k0 + kcb, c0:c0 + cw], writes=[("pw32", sl)], dsem=("pw32", sl))
                            B.cp("dve" if sl == 0 else "pool", W16[sl][:, 0:kcb, 0:cw], W32[sl][:, 0:kcb, 0:cw], [("pw32", sl)], [("pw16", sl)])
                            B.dma("pq", dstv[:, k0:k0 + kcb, c0:c0 + cw], W16[sl][:, 0:kcb, 0:cw], reads=[("pw16", sl)], dsem=("pwst", sl))
                        chunks.append(f)

            add_tiled(w_of, H * DH, 0, D, WOF_S, 0)
            add_tiled(w_od, H * DH, 0, D, WOD_S, 0)
            add_tiled(w_in, D, C_GA, C_GA + 2 * D, WG_S, 0)
            add_nat(w_o, D, D, WO_S)
            add_tiled(w_fi, D, 0, 2 * DFF, WFI_S, 0)
            add_nat(w_fo, DFF, D, WFO_S)
            return chunks

        def head_norm(e_, ps_ap, pkey, gain_ap, out_ap, okey, nfeat, ncols, SQ, SD, tag):
            i = B.ring("hn" + tag, 2)
            sq = SQ[i][:, 0:ncols]; sd = SD[i][:, 0:ncols]
            B.act(sq, ps_ap, AF.Square, [pkey], [("sq" + tag, i)])

            def post():
                B.mm(PA[:, 0:ncols], ONESB[:], sq, True, True, [("sq" + tag, i), "ONESB"], ["pa"])
                B.act(sd, PA[:, 0:ncols], AF.Sqrt, ["pa"], [("sd" + tag, i)], scale=1.0 / nfeat, bias=EPS)
                B.recip(sd, sd, [("sd" + tag, i)], [("sd" + tag, i)])
                B.stt(out_ap, ps_ap, gain_ap, sd, ALU.mult, ALU.mult, [pkey, ("sd" + tag, i), "GAINS"], [okey])
            return post

        with contextlib.ExitStack() as eA:
            WKV = B.sb(eA, "WKV", [128, DC, 2440], BF16)
            WUKV = B.sb(eA, "WUKV", [128, 2, 2048], BF16)
            with contextlib.ExitStack() as e1:
                W32 = [B.sb(e1, "wa32_%d" % i, [128, DC, 512], F32) for i in range(2)]
                i = 0
                for (slo, cw, dlos) in ((C_KF, 512, (0,)), (C_KF + 512, 512, (512,)), (C_CKV, 256, (1024,)), (C_KI, 64, (1280, 1344)),
                                        (C_VF, 512, (1408,)), (C_VF + 512, 512, (1920,)), (C_FG, 8, (2432,))):
                    sl = i % 2
                    B.dma("sp", W32[sl][:, :, 0:cw], w_in_v[:, :, slo:slo + cw], writes=[("wa32", sl)], dsem=("wa32", sl))
                    for dlo in dlos:
                        B.cp("dve" if i % 2 == 0 else "pool", WKV[:, :, dlo:dlo + cw], W32[sl][:, :, 0:cw], [("wa32", sl)], ["WKV"])
                    i += 1
                w_ukv_v = w_ukv.rearrange("(k p) n -> p k n", p=128)
                for c0 in range(0, 2048, 1024):
                    sl = i % 2
                    wv = W32[sl][:, 0:4, :].rearrange("p k n -> p (k n)").rearrange("p (k n) -> p k n", k=2)
                    B.dma("sp", wv, w_ukv_v[:, :, c0:c0 + 1024], writes=[("wa32", sl)], dsem=("wa32", sl))
                    B.cp("dve", WUKV[:, :, c0:c0 + 1024], wv, [("wa32", sl)], ["WUKV"])
                    i += 1
            P.consts.update(["WKV", "WUKV"])
            BFG = B.sb(eA, "BFG", [128, 8], F32)
            CARRY = B.sb(eA, "CARRY", [128, 2, 8], F32)
            XT = [B.sb(eA, "XT%d" % i, [128, D], F32) for i in range(2)]
            XN = [B.sb(eA, "XN%d" % i, [128, D], BF16) for i in range(2)]
            HT = [B.sb(eA, "HT%d" % i, [128, DC, 512], BF16) for i in range(2)]
            SQ = [B.sb(eA, "SQ%d" % i, [128, 512], BF16) for i in range(2)]
            SD = [B.sb(eA, "SD%d" % i, [128, 512], F32) for i in range(2)]
            CKVN = [B.sb(eA, "CKVN%d" % i, [128, 2, 512], BF16) for i in range(2)]
            KST = [B.sb(eA, "KST%d" % i, [128, 8, 512], BF16) for i in range(2)]
            KIST = [B.sb(eA, "KIST%d" % i, [128, 512], BF16) for i in range(2)]
            VST = [B.sb(eA, "VST%d" % i, [128, 1024], BF16) for i in range(2)]
            LG = [B.sb(eA, "LG%d" % i, [128, 3, 8], F32) for i in range(2)]
            B.dma("sp", BFG[:], b_forget.partition_broadcast(128)[:, 0, :], writes=["BFG"], dsem="c_BFG")
            B.memset("pool", CARRY[:, 0, :], 0.0, [("carry", 0)])
            kf_v = KF_T.rearrange("h p t -> p h t"); kd_v = KD_T.rearrange("h p t -> p h t")
            for g in range(ng_a):
                hs = B.ring("htA", 2)
                htk = ("htA", hs)
                for t in range(4):
                    kb = 4 * g + t
                    xs = B.ring("xtA", 2)
                    B.dma("sp", XT[xs][:], xb[kb * 128:(kb + 1) * 128, :], writes=[("xtA", xs)], dsem=("xtA", xs))
                    ns = B.ring("xnA", 2)
                    norm_tile(XT[xs][:], ("xtA", xs), 0, HT[hs], htk, t * 128, XN[ns], ("xnA", ns), B.ring("stat", 8))
                ks = B.ring("kstA", 2)
                pend = None
                for h in range(H):
                    pst, pk = MMR[B.ring("mmr", 4)]
                    for dc in range(DC):
                        B.mm(pst, WKV[:, dc, h * 128:(h + 1) * 128], HT[hs][:, dc, :], dc == 0, dc == DC - 1, ["WKV", htk], [pk])
                    post = head_norm(eA, pst, pk, GAINS[:, 1:2], KST[ks][:, h, :], ("kstA", ks), 128, 512, SQ, SD, "A")
                    if pend:
                        pend()
                    pend = post
                pend()
                B.dma("pq", kf_v[:, :, g * 512:(g + 1) * 512], KST[ks][:], reads=[("kstA", ks)], dsem=("kstA", ks))
                pst, pk = MMR[B.ring("mmr", 4)]
                for dc in range(DC):
                    B.mm(pst, WKV[:, dc, 1280:1408], HT[hs][:, dc, :], dc == 0, dc == DC - 1, ["WKV", htk], [pk])
                kis = B.ring("kistA", 2)
                B.cp("act", KIST[kis][:], pst, [pk], [("kistA", kis)])
                B.dma("pq", KI2_T[:, g * 512:(g + 1) * 512], KIST[kis][:], reads=[("kistA", kis)], dsem=("kistA", kis))
                cs = B.ring("ckvnA", 2)
                ckk = ("ckvnA", cs)
                cps = []
                for fc in range(2):
                    pst, pk = MMR[B.ring("mmr", 4)]
                    for dc in range(DC):
                        B.mm(pst, WKV[:, dc, 1024 + fc * 128:1024 + (fc + 1) * 128], HT[hs][:, dc, :], dc == 0, dc == DC - 1, ["WKV", htk], [pk])
                    cps.append((pst, pk))
                for fc in range(2):
                    qi_ = B.ring("hnA", 2)
                    B.act(SQ[qi_][:], cps[fc][0], AF.Square, [cps[fc][1]], [("sqA", qi_)])
                    B.mm(PA[:, :], ONESB[:], SQ[qi_][:], fc == 0, fc == 1, [("sqA", qi_), "ONESB"], ["pa"])
                B.act(SD[qi_][:], PA[:, :], AF.Sqrt, ["pa"], [("sdA", qi_)], scale=1.0 / KVL, bias=EPS)
                B.recip(SD[qi_][:], SD[qi_][:], [("sdA", qi_)], [("sdA", qi_)])
                for fc in range(2):
                    B.stt(CKVN[cs][:, fc, :], cps[fc][0], GAINS[:, 4 + fc:5 + fc], SD[qi_][:], ALU.mult, ALU.mult,
                          [cps[fc][1], ("sdA", qi_), "GAINS"], [ckk])
                ks = B.ring("kstA", 2)
                pend = None
                for h in range(H):
                    pst, pk = MMR[B.ring("mmr", 4)]
                    for fc in range(2):
                        B.mm(pst, WUKV[:, fc, h * 128:(h + 1) * 128], CKVN[cs][:, fc, :], fc == 0, fc == 1, ["WUKV", ckk], [pk])
                    post = head_norm(eA, pst, pk, GAINS[:, 3:4], KST[ks][:, h, :], ("kstA", ks), 128, 512, SQ, SD, "A")
                    if pend:
                        pend()
                    pend = post
                pend()
                B.dma("pq", kd_v[:, :, g * 512:(g + 1) * 512], KST[ks][:], reads=[("kstA", ks)], dsem=("kstA", ks))
                for t in range(4):
                    kb = 4 * g + t
                    tc = slice(t * 128, (t + 1) * 128)
                    vs = B.ring("vstA", 2)
                    for cc in range(2):
                        pst, pk = MMR[B.ring("mmr", 4)]
                        for dc in range(DC):
                            B.mm(pst, HT[hs][:, dc, tc], WKV[:, dc, 1408 + cc * 512:1408 + (cc + 1) * 512], dc == 0, dc == DC - 1, ["WKV", htk], [pk])
                        B.cp("act" if cc == 0 else "dve", VST[vs][:, cc * 512:(cc + 1) * 512], pst, [pk], [("vstA", vs)])
                    B.dma("pq", VF[:, :, kb, :].rearrange("h p d -> p h d"), VST[vs][:].rearrange("p (h d) -> p h d", d=128),
                          reads=[("vstA", vs)], dsem=("vstA", vs))
                    vs = B.ring("vstA", 2)
                    for cc in range(2):
                        pst, pk = MMR[B.ring("mmr", 4)]
                        for fc in range(2):
                            B.mm(pst, CKVN[cs][:, fc, tc], WUKV[:, fc, 1024 + cc * 512:1024 + (cc + 1) * 512], fc == 0, fc == 1, ["WUKV", ckk], [pk])
                        B.cp("act" if cc == 0 else "dve", VST[vs][:, cc * 512:(cc + 1) * 512], pst, [pk], [("vstA", vs)])
                    B.dma("pq", VD[:, :, kb, :].rearrange("h p d -> p h d"), VST[vs][:].rearrange("p (h d) -> p h d", d=128),
                          reads=[("vstA", vs)], dsem=("vstA", vs))
                    for dc in range(DC):
                        B.mm(PBK[:, 0:8], HT[hs][:, dc, tc], WKV[:, dc, 2432:2440], dc == 0, dc == DC - 1, ["WKV", htk], ["pbk"])
                    ls = B.ring("lgA", 2)
                    lk = ("lgA", ls)
                    B.tt("dve", LG[ls][:, 0, :], PBK[:, 0:8], BFG[:], ALU.add, ["pbk", "BFG"], [lk])
                    B.act(LG[ls][:, 1, :], LG[ls][:, 0, :], AF.Exp, [lk], [lk], scale=-1.0)
                    B.act(LG[ls][:, 2, :], LG[ls][:, 1, :], AF.Ln, [lk], [lk], bias=1.0)
                    B.mm(PBK[:, 8:16], TRIF[:], LG[ls][:, 2, :], True, True, ["TRIF", lk], ["pbk"])
                    B.mm(PBK[:, 16:24], ONESF[:], LG[ls][:, 2, :], True, True, ["ONESF", lk], ["pbk"])
                    cur = kb % 2
                    B.tt("dve", CUML[:, kb, :], PBK[:, 8:16], CARRY[:, cur, :], ALU.add, ["pbk", ("carry", cur)], ["CUML"])
                    B.tt("dve", CARRY[:, 1 - cur, :], PBK[:, 16:24], CARRY[:, cur, :], ALU.add, ["pbk", ("carry", cur)], [("carry", 1 - cur)])
            if dbg:
                d_cum = dbg_t("cuml", [128, NKB * 8])
                B.dma("sp", d_cum[:, :], CUML[:].rearrange("p a b -> p (a b)"), reads=["CUML"], dsem="dbg")
        P.consts.update(["CUML"])
        if dbg:
            for nm, src, shp in (("kf", KF_T[0, :, 0:1024], [128, 1024]), ("kd", KD_T[7, :, 0:1024], [128, 1024]),
                                 ("ki", KI2_T[:, 0:1024], [128, 1024])):
                d_ = dbg_t(nm, shp, BF16)
                B.dma("sp", d_[:, :], src, dsem="dbg")
            for nm, src in (("vf", VF[1, :, 0:8, :]), ("vd", VD[6, :, 0:8, :])):
                d_ = dbg_t(nm, [128, 8, 128], BF16)
                B.dma("sp", d_[:, :, :], src, dsem="dbg")

        P.barrier_dma("sp", st_sems)
        with contextlib.ExitStack() as eB:
            WQ = B.sb(eB, "WQ", [128, DC, 3088], BF16)
            XT = [B.sb(eB, "XTb%d" % i, [128, D], F32) for i in range(2)]
            XN = [B.sb(eB, "XNb%d" % i, [128, D], BF16) for i in range(2)]
            HT = [B.sb(eB, "HTb%d" % i, [128, DC, 512], BF16) for i in range(2)]
            SQ = [B.sb(eB, "SQb%d" % i, [128, 512], BF16) for i in range(2)]
            SD = [B.sb(eB, "SDb%d" % i, [128, 512], F32) for i in range(2)]
            QST = [B.sb(eB, "QST%d" % i, [128, 8, 512], BF16) for i in range(2)]
            WIST = [B.sb(eB, "WIST%d" % i, [128, 16], F32) for i in range(2)]
            for k4 in range(4):
                B.dma("sp", WQ[:, k4 * 4:(k4 + 1) * 4, :], wq_v[:, k4 * 4:(k4 + 1) * 4, :], writes=["WQ"], dsem="c_WQ")
            P.consts.add("WQ")
            for gi in range(NSLOT + 1):
                ntile = 4 if gi < NSLOT else 1
                ncols = ntile * 128
                tok0 = gi * 512
                hs = B.ring("htB", 2); htk = ("htB", hs)
                for t in range(ntile):
                    xs = B.ring("xtB", 2)
                    B.dma("sp", XT[xs][:], xo[tok0 + t * 128:tok0 + (t + 1) * 128, :], writes=[("xtB", xs)], dsem=("xtB", xs))
                    ns = B.ring("xnB", 2)
                    norm_tile(XT[xs][:], ("xtB", xs), 0, HT[hs], htk, t * 128, XN[ns], ("xnB", ns), B.ring("stat", 8))
                for (seg, dst, gcol) in ((0, QF_T, 0), (1024, QD_T, 2)):
                    ks = B.ring("qstB", 2)
                    pend = None
                    for h in range(H):
                        pst, pk = MMR[B.ring("mmr", 4)]
                        for dc in range(DC):
                            B.mm(pst[:, 0:ncols], WQ[:, dc, seg + h * 128:seg + (h + 1) * 128], HT[hs][:, dc, 0:ncols], dc == 0, dc == DC - 1, ["WQ", htk], [pk])
                        post = head_norm(eB, pst[:, 0:ncols], pk, GAINS[:, gcol:gcol + 1], QST[ks][:, h, 0:ncols], ("qstB", ks), 128, ncols, SQ, SD, "B")
                        if pend:
                            pend()
                        pend = post
                    pend()
                    B.dma("pq", dst.rearrange("h p t -> p h t")[:, :, tok0:tok0 + ncols], QST[ks][:, :, 0:ncols], reads=[("qstB", ks)], dsem=("qstB", ks))
                ks = B.ring("qstB", 2)
                for m in range(8):
                    pst, pk = MMR[B.ring("mmr", 4)]
                    for dc in range(DC):
                        B.mm(pst[:, 0:ncols], WQ[:, dc, 2048 + m * 128:2048 + (m + 1) * 128], HT[hs][:, dc, 0:ncols], dc == 0, dc == DC - 1, ["WQ", htk], [pk])
                    B.cp("act" if m % 2 == 0 else "dve", QST[ks][:, m, 0:ncols], pst[:, 0:ncols], [pk], [("qstB", ks)])
                B.dma("pq", QI_T.rearrange("h p t -> p h t")[:, :, tok0:tok0 + ncols], QST[ks][:, :, 0:ncols], reads=[("qstB", ks)], dsem=("qstB", ks))
                for t in range(ntile):
                    for dc in range(DC):
                        B.mm(PBK[:, 0:16], HT[hs][:, dc, t * 128:(t + 1) * 128], WQ[:, dc, 3072:3088], dc == 0, dc == DC - 1, ["WQ", htk], ["pbk"])
                    ws = B.ring("wistB", 2)
                    B.ts("dve", WIST[ws][:], PBK[:, 0:16], WSCALE, None, ALU.mult, None, ["pbk"], [("wistB", ws)])
                    B.dma("pq", WI_S[tok0 + t * 128:tok0 + (t + 1) * 128, :], WIST[ws][:], reads=[("wistB", ws)], dsem=("wistB", ws))
        if dbg:
            for nm, src in (("qf", QF_T[3, :, 0:1024]), ("qd", QD_T[5, :, NSLOT * 512 - 896:NSLOT * 512 + 128]), ("qi", QI_T[2, :, 0:1024])):
                d_ = dbg_t(nm, [128, 1024], BF16)
                B.dma("sp", d_[:, :], src, dsem="dbg")
            d_ = dbg_t("wi", [1024, 16], F32)
            B.dma("sp", d_[:, :], WI_S[0:1024, :], dsem="dbg")

        def t5_starts():
            n = np.arange(0, 400)
            nf = np.maximum(n, 1).astype(np.float32)
            large = 16 + (np.log(nf / np.float32(16)) / np.float32(math.log(128 / 16)) * np.float32(16)).astype(np.int32)
            large = np.minimum(large, 31)
            bk = np.where(n < 16, n, large)
            return [int(np.min(n[bk == b_])) for b_ in range(32)]
        STARTS = t5_starts()

        def qb_info(qb):
            if qb < 32:
                s_ = qb // 4
                return dict(s=s_, c=qb % 4, par=s_ % 2, nkb=16 * s_ + 16, vs=16 * s_, ncols=128, halo=False)
            return dict(s=None, c=None, par=None, nkb=NKB, vs=0, ncols=16, halo=True)

        CH = 8

        if stop_after >= "C":
          with contextlib.ExitStack() as eC:
            IOKP = B.sb(eC, "IOKP", [128, 2048], F32)
            SELH = B.sb(eC, "SELH", [128, 8, 128], BF16)
            BT = B.sb(eC, "BT", [128, H, 2, 128], BF16)
            HBP = B.sb(eC, "HBP", [128, H, NSLOT, 16], BF16)
            POW2 = B.sb(eC, "POW2", [128, NBIS], F32)
            with contextlib.ExitStack() as e1:
                IOI = B.sb(e1, "IOI", [128, 2048], I32)
                BTF = B.sb(e1, "BTF", [128, H, 2, 128], F32)
                GT_ = B.sb(e1, "GT_", [128, 128], F32)
                DELTA = B.sb(e1, "DELTA", [128, 248], F32)
                P.op("pool", lambda e: e.iota(IOI[:], pattern=[[1, 2048]], base=0, channel_multiplier=-1), writes=["IOI"])
                B.cp("dve", IOKP[:], IOI[:], ["IOI"], ["IOKP"])
                for i in range(8):
                    B.ts("dve", SELH[:, i, :], IDB[:], SELO[:, i:i + 1], None, ALU.mult, None, ["IDB", "SELO"], ["SELH"])
                for i in range(NBIS):
                    B.memset("pool", POW2[:, i:i + 1], 2.0 ** -(i + 1), ["POW2"])
                B.tt("dve", DELTA[:], RBR[:, 0:248], RBR[:, 8:256], ALU.subtract, ["RBR"], ["DELTA"])
                B.memset("pool", BTF[:], 0.0, ["BTF"])
                for dl in range(2):
                    for b_ in range(31):
                        B.ts("dve", GT_[:], IOKP[:, 0:128], float(STARTS[b_ + 1] - 128 * dl), None, ALU.is_lt, None, ["IOKP"], ["GT_"])
                        for h in range(H):
                            B.stt(BTF[:, h, dl, :], GT_[:], DELTA[:, b_ * 8 + h:b_ * 8 + h + 1], BTF[:, h, dl, :], ALU.mult, ALU.add,
                                  ["GT_", "DELTA", "BTF"], ["BTF"])
                B.cp("dve", BT[:], BTF[:], ["BTF"], ["BT"])
                B.memset("pool", HBP[:], 0.0, ["HBP"])
                for s_ in range(NSLOT):
                    B.cp("dve", HBP[:, :, s_, 2 * s_:2 * s_ + 2], BT[:, :, 0, 126:128], ["BT", "HBP"], ["HBP"])
            P.consts.update(["IOKP", "SELH", "BT", "HBP", "POW2"])

            with contextlib.ExitStack() as eD:
                SC = B.sb(eD, "SC", [128, L], F32)
                NM = B.sb(eD, "NM", [128, L], BF16)
                MT = B.sb(eD, "MT", [128, L], BF16)
                QI = [B.sb(eD, "QI%d" % i, [128, 8, 128], BF16) for i in range(2)]
                WIQ = [B.sb(eD, "WIQ%d" % i, [128, 16], F32) for i in range(2)]
                DIAG = B.sb(eD, "DIAG", [128, 16, 128], BF16)
                KIC = [B.sb(eD, "KIC%d" % i, [128, 2048], BF16) for i in range(2)]
                RR = [B.sb(eD, "RR%d" % i, [128, 1024], BF16) for i in range(3)]
                BIS = B.sb(eD, "BIS", [128, 8], F32)
                BW = B.sb(eD, "BW", [128, NBIS], F32)
                QT = [B.sb(eD, "QT%d" % i, [128, 8, 128], BF16) for i in range(2)]
                KC = [B.sb(eD, "KC%d" % i, [128, CH * 128], BF16) for i in range(2)]
                VC = [B.sb(eD, "VC%d" % i, [128, CH, 130], BF16) for i in range(2)]
                PT = [B.sb(eD, "PT%d" % i, [128, 512], BF16) for i in range(3)]
                OS = [B.sb(eD, "OS%d" % i, [128, 128], BF16) for i in range(2)]
                RC = [B.sb(eD, "RC%d" % i, [128, 1], F32) for i in range(2)]
                OT = [B.sb(eD, "OT%d" % i, [128, 8, 128], BF16) for i in range(2)]
                for i in range(2):
                    B.memset("pool", VC[i][:, :, 128:129], 1.0, [("vcC1", i)])
                qi_v = QI_T.rearrange("m p t -> p m t"); qd_v = QD_T.rearrange("h p t -> p h t"); attd_v = ATTD_T.rearrange("h p t -> p h t")
                qb_list = list(range(nqb_c)) if nqb_c < NT else list(range(NT))
                if nqb_c < NT and (NT - 1) not in qb_list:
                    qb_list.append(NT - 1)
                PPX = [(PP[0], ("pp0a", "pp0b")), (PP[1], ("pp1a", "pp1b"))]

                class Ctx:
                    pass

                def c_load(qb):
                    cx = Ctx()
                    cx.qb = qb
                    cx.inf = qb_info(qb)
                    cx.nkb, cx.vs, cx.ncols = cx.inf["nkb"], cx.inf["vs"], cx.inf["ncols"]
                    cx.nk = cx.nkb * 128
                    cx.qs = B.ring("qiC", 2)
                    qs = cx.qs
                    B.dma("sp", QI[qs][:], qi_v[:, :, qb * 128:(qb + 1) * 128], writes=[("qiC", qs)], dsem=("qiC", qs))
                    B.dma("sp", WIQ[qs][:], WI_S[qb * 128:(qb + 1) * 128, :], writes=[("wiqC", qs)], dsem=("wiqC", qs))
                    for j_ in range(16):
                        B.ts("dve", DIAG[:, j_, :], IDB[:], WIQ[qs][:, j_:j_ + 1], None, ALU.mult, None, ["IDB", ("wiqC", qs)], ["DIAG"])
                    return cx

                def c1(cx):
                    qs = cx.qs
                    kic_of = {}

                    def c1_pair(kt, m):
                        if kt % 4 == 0 and m == 0:
                            kis_ = B.ring("kicC", 2)
                            B.dma("sp", KIC[kis_][:], KI2_T[:, kt * 512:kt * 512 + 2048], writes=[("kicC", kis_)], dsem=("kicC", kis_))
                            kic_of[kt // 4] = kis_
                        kis_ = kic_of[kt // 4]
                        kc0 = (kt % 4) * 512
                        ppt, pkk = PPX[B.ring("ppC", 2)]
                        B.mm(ppt[:, 0, :], QI[qs][0:64, m, :], KIC[kis_][0:64, kc0:kc0 + 512], True, True, [("qiC", qs), ("kicC", kis_)], [pkk[0]])
                        B.mm(ppt[:, 1, :], QI[qs][64:128, m, :], KIC[kis_][64:128, kc0:kc0 + 512], True, True, [("qiC", qs), ("kicC", kis_)], [pkk[1]])
                        ri = B.ring("rrC", 3)
                        ppv = ppt[:].rearrange("p a b -> p (a b)")
                        if m % 2 == 0:
                            B.act(RR[ri][:], ppv, AF.Relu, list(pkk), [("rrC", ri)])
                        else:
                            B.ts("dve", RR[ri][:], ppv, 0.0, None, ALU.max, None, list(pkk), [("rrC", ri)])
                        return ri

                    def c1_rest(kt, m, ri):
                        acc, ak = ((PA, "pa"), (PBK, "pbk"))[kt % 2]
                        B.mm(acc[:, :], DIAG[:, 2 * m, :], RR[ri][:, 0:512], m == 0, False, ["DIAG", ("rrC", ri)], [ak])
                        B.mm(acc[:, :], DIAG[:, 2 * m + 1, :], RR[ri][:, 512:1024], False, m == 7, ["DIAG", ("rrC", ri)], [ak])
                        if m == 7:
                            B.cp("act" if kt % 2 == 0 else "dve", SC[:, kt * 512:(kt + 1) * 512], acc[:, :], [ak], ["SC"])

                    queue = []
                    for kt in range(cx.nkb // 4):
                        for m in range(8):
                            ri = c1_pair(kt, m)
                            queue.append((kt, m, ri))
                            if len(queue) > 1:
                                c1_rest(*queue.pop(0))
                    for q_ in queue:
                        c1_rest(*q_)

                def c2_steps(cx):
                    inf, nk, vs = cx.inf, cx.nk, cx.vs
                    steps = []

                    def pro():
                        P.op("dve", lambda e: e.tensor_reduce(out=BIS[:, 0:1], in_=SC[:, 0:nk], axis=AX.X, op=ALU.min), reads=["SC"], writes=["BIS"])
                        if not inf["halo"]:
                            cv = CVAL[:, inf["par"] * 4 + inf["c"]:inf["par"] * 4 + inf["c"] + 1]
                            B.ts("dve", NM[:, 0:2048], IOKP[:], cv, 0.0, ALU.subtract, ALU.is_gt, ["IOKP", "CVAL", "NM"], ["NM"])
                            B.stt(SC[:, vs * 128:vs * 128 + 2048], NM[:, 0:2048], -1e30, SC[:, vs * 128:vs * 128 + 2048], ALU.mult, ALU.add, ["NM", "SC"], ["SC"])
                        else:
                            for ch in range(8):
                                B.ts("dve", NM[:, 0:2048], IOKP[:], HP2[:, 0:1], float(-ch * 2048), ALU.add, ALU.is_gt, ["IOKP", "HP2", "NM"], ["NM"])
                                B.stt(SC[:, ch * 2048:(ch + 1) * 2048], NM[:, 0:2048], -1e30, SC[:, ch * 2048:(ch + 1) * 2048], ALU.mult, ALU.add, ["NM", "SC"], ["SC"])
                        P.op("dve", lambda e: e.tensor_reduce(out=BIS[:, 1:2], in_=SC[:, 0:nk], axis=AX.X, op=ALU.max), reads=["SC", "BIS"], writes=["BIS"])
                        B.tt("dve", BIS[:, 2:3], BIS[:, 1:2], BIS[:, 0:1], ALU.subtract, ["BIS"], ["BIS"])
                        B.ts("dve", BW[:], POW2[:], BIS[:, 2:3], None, ALU.mult, None, ["POW2", "BIS"], ["BW"])
                        B.cp("dve", BIS[:, 3:4], BIS[:, 0:1], ["BIS"], ["BIS"])
                    steps.append(pro)

                    def mk_it(it):
                        def f():
                            B.tt("dve", BIS[:, 4:5], BIS[:, 3:4], BW[:, it:it + 1], ALU.add, ["BIS", "BW"], ["BIS"])
                            B.ts("dve", NM[:, 0:nk], SC[:, 0:nk], BIS[:, 4:5], 0.0, ALU.is_ge, ALU.add, ["SC", "BIS", "NM"], ["NM", "BIS"], accum_out=BIS[:, 5:6])
                            B.ts("dve", BIS[:, 6:7], BIS[:, 5:6], float(TOPK), BW[:, it:it + 1], ALU.is_ge, ALU.mult, ["BIS", "BW"], ["BIS"])
                            B.tt("dve", BIS[:, 3:4], BIS[:, 3:4], BIS[:, 6:7], ALU.add, ["BIS"], ["BIS"])
                        return f
                    for it in range(NBIS):
                        steps.append(mk_it(it))

                    def epi():
                        B.ts("dve", NM[:, 0:nk], SC[:, 0:nk], BIS[:, 3:4], None, ALU.is_lt, None, ["SC", "BIS", "NM"], ["NM"])
                    steps.append(epi)
                    return steps

                def c3(cx):
                    ncols = cx.ncols
                    if cx.inf["halo"]:
                        mtv = MT[:, 0:NKB * 16].rearrange("p (a b) -> p a b", b=16)
                    else:
                        mtv = MT[:].rearrange("p (a b) -> p a b", b=128)
                    for kb in range(cx.nkb):
                        hk = ("tpb", (kb // 8) % 2)
                        B.tr(TPB[:, (kb % 16) * 128:(kb % 16 + 1) * 128], NM[:, kb * 128:(kb + 1) * 128], IDB[:], ["NM", "IDB"], [hk])
                        if kb % 8 == 7:
                            half = (kb // 8) % 2
                            src = TPB[:, half * 1024:(half + 1) * 1024].rearrange("p (a b) -> p a b", b=128)[:, :, 0:ncols]
                            B.cp("act" if half == 0 else "dve", mtv[:, kb - 7:kb + 1, :], src, [hk], ["MT"])

                def c4_heads(cx):
                    qb, inf, nkb, vs, ncols = cx.qb, cx.inf, cx.nkb, cx.vs, cx.ncols
                    qts = B.ring("qtC", 2)
                    B.dma("sp", QT[qts][:], qd_v[:, :, qb * 128:(qb + 1) * 128], writes=[("qtC", qts)], dsem=("qtC", qts))
                    ots = B.ring("otC", 2)
                    cx.ots = ots
                    if inf["halo"]:
                        B.memset("pool", OT[ots][:], 0.0, [("otC", ots)])
                    cand = {}
                    if not inf["halo"]:
                        for dl in range(2):
                            for i in range(4):
                                kb = vs + inf["c"] - dl + 4 * i
                                if kb >= 0:
                                    cand.setdefault(kb, []).append((inf["par"] * 4 + i, dl))
                    else:
                        for s_ in range(NSLOT):
                            for i in range(4):
                                kb = 16 * s_ + 4 * i - 1
                                if kb >= 0:
                                    cand.setdefault(kb, []).append(((s_ % 2) * 4 + i, s_))
                    cx.pend = None

                    def mk_head(h):
                        def f():
                            ai = B.ring("accV", 2)
                            accv, avk = ((PA, "pa"), (PBK, "pbk"))[ai]
                            for ck in range(nkb // CH):
                                kvs = B.ring("kvC", 2)
                                B.dma("sp", KC[kvs][:], KD_T[h, :, ck * CH * 128:(ck + 1) * CH * 128], writes=[("kcC", kvs)], dsem=("kcC", kvs))
                                B.dma("sp", VC[kvs][:, :, 0:128], VD[h, :, ck * CH:(ck + 1) * CH, :], writes=[("vcC", kvs)], dsem=("vcC", kvs))
                                for tl in range(CH // 4):
                                    kb0 = ck * CH + tl * 4
                                    S, sk = MMR[B.ring("mmr", 4)]
                                    W_ = 4 * ncols
                                    B.mm(S[:, 0:W_], NEGI[:], MT[:, kb0 * ncols:(kb0 + 4) * ncols], True, False, ["NEGI", "MT"], [sk])
                                    for i in range(4):
                                        for (si, x_) in cand.get(kb0 + i, ()):
                                            rhs = HBP[:, h, x_, :] if inf["halo"] else BT[:, h, x_, :]
                                            B.mm(S[:, i * ncols:(i + 1) * ncols], SELH[:, si, :], rhs, False, False, ["SELH", "BT", "HBP"], [sk])
                                    for i in range(4):
                                        B.mm(S[:, i * ncols:(i + 1) * ncols], KC[kvs][:, (tl * 4 + i) * 128:(tl * 4 + i + 1) * 128], QT[qts][:, h, 0:ncols],
                                             False, i == 3, [("kcC", kvs), ("qtC", qts)], [sk])
                                    ps_ = B.ring("ptC", 3)
                                    B.act(PT[ps_][:, 0:W_], S[:, 0:W_], AF.Exp, [sk, "RBR"], [("ptC", ps_)], bias=RBR[:, 248 + h:249 + h])

                                    def pv(ps_=ps_, kb0=kb0, kvs=kvs, tl=tl, accv=accv, avk=avk):
                                        for i in range(4):
                                            kb = kb0 + i
                                            B.mm(accv[0:ncols, 0:129], PT[ps_][:, i * ncols:(i + 1) * ncols], VC[kvs][:, tl * 4 + i, 0:129], kb == 0, kb == nkb - 1,
                                                 [("ptC", ps_), ("vcC", kvs), ("vcC1", kvs)], [avk])
                                        if kb0 + 4 == nkb:
                                            oi = B.ring("osC", 2)
                                            B.recip(RC[oi][0:ncols, :], accv[0:ncols, 128:129], [avk], [("rcC", oi)])
                                            B.ts("dve", OS[oi][0:ncols, :], accv[0:ncols, 0:128], RC[oi][0:ncols, 0:1], None, ALU.mult, None, [avk, ("rcC", oi)], [("osC", oi)])
                                            B.tr(TPB[:, 1024 + oi * 128:1024 + oi * 128 + ncols], OS[oi][0:ncols, :], IDB[0:ncols, 0:ncols], [("osC", oi), "IDB"], [("tpb", 1)])
                                            B.cp("act", OT[ots][:, h, 0:ncols], TPB[:, 1024 + oi * 128:1024 + oi * 128 + ncols], [("tpb", 1)], [("otC", ots)])
                                    if cx.pend is not None:
                                        cx.pend()
                                    cx.pend = pv
                        return f
                    return [mk_head(h) for h in range(H)]

                def c4_finish(cx):
                    if cx.pend is not None:
                        cx.pend()
                        cx.pend = None
                    B.dma("pq", attd_v[:, :, cx.qb * 128:(cx.qb + 1) * 128], OT[cx.ots][:], reads=[("otC", cx.ots)], dsem=("otC", cx.ots))

                cur = c_load(qb_list[0])
                c1(cur)
                for f_ in c2_steps(cur):
                    f_()
                c3(cur)
                for idx in range(len(qb_list)):
                    nxt = None
                    steps = []
                    if idx + 1 < len(qb_list):
                        nxt = c_load(qb_list[idx + 1])
                        c1(nxt)
                        steps = c2_steps(nxt)
                    heads = c4_heads(cur)
                    k_ = 0
                    for hi, hf in enumerate(heads):
                        hf()
                        tgt = (len(steps) * (hi + 1)) // len(heads)
                        while k_ < tgt:
                            steps[k_]()
                            k_ += 1
                    c4_finish(cur)
                    if nxt is not None:
                        c3(nxt)
                    cur = nxt

            if stop_after >= "D":
              with contextlib.ExitStack() as eD:
                FM = B.sb(eD, "FM", [128, 4, 16 * 128], BF16)
                HFM = B.sb(eD, "HFM", [128, NKB * 16], BF16)
                SELIF = B.sb(eD, "SELIF", [128, 8, 128], F32)
                OH8 = B.sb(eD, "OH8", [8, 8, 128], BF16)
                CQ4 = [B.sb(eD, "CQ4_%d" % i, [8, 512], BF16) for i in range(4)]
                CQH = B.sb(eD, "CQH", [8, 64], BF16)
                QF = [B.sb(eD, "QF%d" % i, [128, 8, 512], BF16) for i in range(2)]
                KC = [B.sb(eD, "KCd%d" % i, [128, CH * 128], BF16) for i in range(2)]
                VC = [B.sb(eD, "VCd%d" % i, [128, CH, 130], BF16) for i in range(2)]
                PT = [B.sb(eD, "PTd%d" % i, [128, 512], BF16) for i in range(3)]
                OS = [B.sb(eD, "OSd%d" % i, [128, 128], BF16) for i in range(2)]
                RC = [B.sb(eD, "RCd%d" % i, [128, 1], F32) for i in range(2)]
                OT = [B.sb(eD, "OTd%d" % i, [128, 8, 512], BF16) for i in range(2)]
                for i in range(2):
                    B.memset("pool", VC[i][:, :, 128:129], 1.0, [("vcD1", i)])
                PW32 = [B.sb(eD, "PW32_%d" % i, [128, DC, 512], F32) for i in range(2)]
                PW16 = [B.sb(eD, "PW16_%d" % i, [128, DC, 512], BF16) for i in range(2)]
                prep_chunks = prep_chunk_list(PW32, PW16) if stop_after >= "E" else []
                prep_state = {"n": 0, "k": 0}
                for i in range(8):
                    B.ts("dve", SELIF[:, i, :], IDF[:], SELO[:, i:i + 1], None, ALU.mult, None, ["IDF", "SELO"], ["SELIF"])
                    B.ts("dve", OH8[:, i, :], ONESB[0:8, :], IDF[0:8, i:i + 1], None, ALU.mult, None, ["ONESB", "IDF"], ["OH8"])
                for kb in range(NKB):
                    B.ts("dve", HFM[:, kb * 16:(kb + 1) * 16], HPOSR[:], IOKP[:, 0:1], float(128 * kb), ALU.add, ALU.is_lt, ["HPOSR", "IOKP"], ["HFM"])
                P.consts.update(["SELIF", "OH8", "HFM"])
                qf_v = QF_T.rearrange("h p t -> p h t"); attf_v = ATTF_T.rearrange("h p t -> p h t")
                ACCS = [(PP[1][:, 0, 0:129], "pp1a"), (PP[1][:, 1, 0:129], "pp1b"), (PA[:, 0:129], "pa"), (PBK[:, 0:129], "pbk")]
                slots = list(range(nslot_d)) + [NSLOT]
                for sl_ in slots:
                    halo = sl_ == NSLOT
                    if not halo:
                        par = sl_ % 2; vs = 16 * sl_; nkb = vs + 16; ncols = 128; nq = 4
                    else:
                        par = None; vs = 0; nkb = NKB; ncols = 16; nq = 1
                    qfs = B.ring("qfD", 2)
                    ntok = 512 if not halo else 128
                    B.dma("sp", QF[qfs][:, :, 0:ntok], qf_v[:, :, sl_ * 512:sl_ * 512 + ntok], writes=[("qfD", qfs)], dsem=("qfD", qfs))
                    if not halo:
                        for c in range(4):
                            for r in range(16):
                                B.ts("dve", FM[:, c, r * 128:(r + 1) * 128], IOKP[:, 0:128], CVAL[:, par * 4 + c:par * 4 + c + 1], float(128 * r),
                                     ALU.add, ALU.is_lt, ["IOKP", "CVAL"], ["FM"])
                            for i in range(4):
                                B.mm(PP[0][0:8, 0, 0:128], CUML[:, vs + 4 * i + c, :], SELIF[:, par * 4 + i, :], i == 0, i == 3, ["CUML", "SELIF"], ["pp0a"])
                            for j_ in range(4):
                                B.ts("dve", CQ4[c][:, j_ * 128:(j_ + 1) * 128], PP[0][0:8, 0, 0:128], -1.0, None, ALU.mult, None, ["pp0a"], [("cq4", c)])
                    else:
                        first = True
                        lst = [(s_, i) for s_ in range(NSLOT) for i in range(4) if 16 * s_ + 4 * i - 1 >= 0]
                        for n_, (s_, i) in enumerate(lst):
                            kb = 16 * s_ + 4 * i - 1
                            B.mm(PP[0][0:8, 0, 0:16], CUML[:, kb, :], HSELT[:, (s_ * 4 + i) * 16:(s_ * 4 + i + 1) * 16], n_ == 0, n_ == len(lst) - 1,
                                 ["CUML", "HSELT"], ["pp0a"])
                        for j_ in range(4):
                            B.ts("dve", CQH[:, j_ * 16:(j_ + 1) * 16], PP[0][0:8, 0, 0:16], -1.0, None, ALU.mult, None, ["pp0a"], ["cqh"])
                    ots = B.ring("otD", 2)
                    if halo:
                        B.memset("pool", OT[ots][:, :, 0:128], 0.0, [("otD", ots)])
                    pend = None
                    for h in range(H):
                        for ck in range(nkb // CH):
                            kvs = B.ring("kvD", 2)
                            B.dma("sp", KC[kvs][:], KF_T[h, :, ck * CH * 128:(ck + 1) * CH * 128], writes=[("kcD", kvs)], dsem=("kcD", kvs))
                            B.dma("sp", VC[kvs][:, :, 0:128], VF[h, :, ck * CH:(ck + 1) * CH, :], writes=[("vcD", kvs)], dsem=("vcD", kvs))
                            for c in range(nq):
                                accv, avk = ACCS[c]
                                for tl in range(CH // 4):
                                    kb0 = ck * CH + tl * 4
                                    S, sk = MMR[B.ring("mmrD", 2)]
                                    W_ = 4 * ncols
                                    if not halo:
                                        B.mm(S[:, 0:W_], OH8[:, h, :], CQ4[c][:, :], True, False, ["OH8", ("cq4", c)], [sk])
                                        if kb0 >= vs:
                                            B.mm(S[:, 0:W_], NEGI[:], FM[:, c, (kb0 - vs) * 128:(kb0 - vs + 4) * 128], False, False, ["NEGI", "FM"], [sk])
                                        qap = QF[qfs][:, h, c * 128:(c + 1) * 128]
                                    else:
                                        B.mm(S[:, 0:W_], OH8[:, h, :], CQH[:, :], True, False, ["OH8", "cqh"], [sk])
                                        B.mm(S[:, 0:W_], NEGI[:], HFM[:, kb0 * 16:(kb0 + 4) * 16], False, False, ["NEGI", "HFM"], [sk])
                                        qap = QF[qfs][:, h, 0:16]
                                    for i in range(4):
                                        B.mm(S[:, i * ncols:(i + 1) * ncols], KC[kvs][:, (tl * 4 + i) * 128:(tl * 4 + i + 1) * 128], qap,
                                             False, i == 3, [("kcD", kvs), ("qfD", qfs)], [sk])
                                    ps_ = B.ring("ptD", 3)
                                    for i in range(4):
                                        B.act(PT[ps_][:, i * ncols:(i + 1) * ncols], S[:, i * ncols:(i + 1) * ncols], AF.Exp, [sk, "CUML"], [("ptD", ps_)],
                                              bias=CUML[:, kb0 + i, h:h + 1])

                                    def pv(ps_=ps_, kb0=kb0, kvs=kvs, tl=tl, accv=accv, avk=avk, h=h, c=c):
                                        for i in range(4):
                                            kb = kb0 + i
                                            B.mm(accv[0:ncols, :], PT[ps_][:, i * ncols:(i + 1) * ncols], VC[kvs][:, tl * 4 + i, 0:129], kb == 0, kb == nkb - 1,
                                                 [("ptD", ps_), ("vcD", kvs), ("vcD1", kvs)], [avk])
                                        if kb0 + 4 == nkb:
                                            oi = B.ring("osD", 2)
                                            B.recip(RC[oi][0:ncols, :], accv[0:ncols, 128:129], [avk], [("rcD", oi)])
                                            B.ts("dve", OS[oi][0:ncols, :], accv[0:ncols, 0:128], RC[oi][0:ncols, 0:1], None, ALU.mult, None, [avk, ("rcD", oi)], [("osD", oi)])
                                            B.tr(TPB[:, oi * 128:oi * 128 + ncols], OS[oi][0:ncols, :], IDB[0:ncols, 0:ncols], [("osD", oi), "IDB"], [("tpb", 0)])
                                            B.cp("dve", OT[ots][:, h, c * 128:c * 128 + ncols], TPB[:, oi * 128:oi * 128 + ncols], [("tpb", 0)], [("otD", ots)])
                                    if pend is not None:
                                        pend()
                                    pend = pv
                                    prep_state["n"] += 1
                                    if prep_state["n"] % 40 == 0 and prep_state["k"] < len(prep_chunks):
                                        prep_chunks[prep_state["k"]]()
                                        prep_state["k"] += 1
                    pend()
                    B.dma("pq", attf_v[:, :, sl_ * 512:sl_ * 512 + ntok], OT[ots][:, :, 0:ntok], reads=[("otD", ots)], dsem=("otD", ots))
                    if sl_ == slots[-1]:
                        while prep_state["k"] < len(prep_chunks):
                            prep_chunks[prep_state["k"]]()
                            prep_state["k"] += 1
                    if dbg and sl_ in (0, NSLOT):
                        d_ = dbg_t("otf%d" % sl_, [128, 8 * 512], BF16)
                        B.dma("sp", d_[:, :], OT[ots][:].rearrange("p a b -> p (a b)"), reads=[("otD", ots)], dsem="dbg")

        if stop_after >= "F":
          with contextlib.ExitStack() as eE:
            CONVW = B.sb(eE, "CONVW", [128, 3, 2 * FC], F32); CONVB = B.sb(eE, "CONVB", [128, 2 * FC], F32)
            UH = B.sb(eE, "UH", [128, 2 * FC, 16], F32)
            XT = [B.sb(eE, "XTe%d" % i, [128, D], F32) for i in range(4)]
            H2T = B.sb(eE, "H2T", [128, DC, 512], BF16)
            GC = [B.sb(eE, "GC%d" % i, [128, 512], F32) for i in range(2)]
            B.dma("sp", CONVW[:], convw[:, :, :], writes=["CONVW"], dsem="c_CONVW")
            B.dma("sp", CONVB[:], convb[:, :], writes=["CONVB"], dsem="c_CONVB")
            P.consts.update(["CONVW", "CONVB"])
            af_v = ATTF_T.rearrange("h p t -> p h t"); ad_v = ATTD_T.rearrange("h p t -> p h t")
            wo_v = WO_S.rearrange("(k p) n -> p k n", p=128); wfo_v = WFO_S.rearrange("(k p) n -> p k n", p=128)

            def phase_E(gi):
                ntile = 4 if gi < NSLOT else 1
                ncols = ntile * 128
                tok0 = gi * 512
                with contextlib.ExitStack() as e1:
                    HT = B.sb(e1, "HTe", [128, DC, 512], BF16)
                    AFT = B.sb(e1, "AFT", [128, 8, 512], BF16); ADT = B.sb(e1, "ADT", [128, 8, 512], BF16)
                    XN = [B.sb(e1, "XNe%d" % i, [128, D], BF16) for i in range(2)]
                    WS = [B.sb(e1, "WSe%d" % i, [128, 48, 128], BF16) for i in range(3)]
                    MGT = B.sb(e1, "MGT", [128, DC, 512], BF16)
                    SG = [B.sb(e1, "SGe%d" % i, [128, 512], F32) for i in range(4)]
                    WOC = [B.sb(e1, "WOC%d" % i, [128, DC, 256], BF16) for i in range(2)]
                    TM = [B.sb(e1, "TMe%d" % i, [128, 256], F32) for i in range(2)]
                    B.dma("sp", AFT[:, :, 0:ncols], af_v[:, :, tok0:tok0 + ncols], writes=["AFT"], dsem="AFT")
                    B.dma("sp", ADT[:, :, 0:ncols], ad_v[:, :, tok0:tok0 + ncols], writes=["ADT"], dsem="ADT")
                    for t in range(ntile):
                        B.dma("sp", XT[t][:], xo[tok0 + t * 128:tok0 + (t + 1) * 128, :], writes=[("xtE", t)], dsem=("xtE", t))
                        ns = B.ring("xnE", 2)
                        norm_tile(XT[t][:], ("xtE", t), 0, HT, "HTe", t * 128, XN[ns], ("xnE", ns), B.ring("stat", 8))
                    for dcx in range(DC):
                        ws = B.ring("wsE", 3); wk = ("wsE", ws)
                        B.dma("sp", WS[ws][:, 0:8, :], WOF_S[dcx, :, :, :], writes=[wk], dsem=wk)
                        B.dma("sp", WS[ws][:, 8:16, :], WOD_S[dcx, :, :, :], writes=[wk], dsem=wk)
                        B.dma("sp", WS[ws][:, 16:32, :], WG_S[dcx, :, :, :], writes=[wk], dsem=wk)
                        B.dma("sp", WS[ws][:, 32:48, :], WG_S[16 + dcx, :, :, :], writes=[wk], dsem=wk)
                        (pyf, kyf), (pyd, kyd), (pga, kga), (pgb, kgb) = MMR
                        for h in range(H):
                            B.mm(pyf[:, 0:ncols], WS[ws][:, h, :], AFT[:, h, 0:ncols], h == 0, h == H - 1, [wk, "AFT"], [kyf])
                        for h in range(H):
                            B.mm(pyd[:, 0:ncols], WS[ws][:, 8 + h, :], ADT[:, h, 0:ncols], h == 0, h == H - 1, [wk, "ADT"], [kyd])
                        for dc in range(DC):
                            B.mm(pga[:, 0:ncols], WS[ws][:, 16 + dc, :], HT[:, dc, 0:ncols], dc == 0, dc == DC - 1, [wk, "HTe"], [kga])
                        for dc in range(DC):
                            B.mm(pgb[:, 0:ncols], WS[ws][:, 32 + dc, :], HT[:, dc, 0:ncols], dc == 0, dc == DC - 1, [wk, "HTe"], [kgb])
                        B.act(SG[0][:, 0:ncols], pga[:, 0:ncols], AF.Sigmoid, [kga], [("sgE", 0)])
                        B.act(SG[1][:, 0:ncols], pgb[:, 0:ncols], AF.Sigmoid, [kgb], [("sgE", 1)])
                        B.tt("dve", SG[2][:, 0:ncols], pyf[:, 0:ncols], SG[0][:, 0:ncols], ALU.mult, [kyf, ("sgE", 0)], [("sgE", 2)])
                        B.tt("dve", SG[3][:, 0:ncols], pyd[:, 0:ncols], SG[1][:, 0:ncols], ALU.mult, [kyd, ("sgE", 1)], [("sgE", 3)])
                        B.tt("pool", MGT[:, dcx, 0:ncols], SG[2][:, 0:ncols], SG[3][:, 0:ncols], ALU.add, [("sgE", 2), ("sgE", 3)], ["MGT"])
                    for cc in range(8):
                        wc = B.ring("wocE", 2); wck = ("wocE", wc)
                        B.dma("sp", WOC[wc][:], wo_v[:, :, cc * 256:(cc + 1) * 256], writes=[wck], dsem=wck)
                        gs_ = B.ring("gcE", 2); gk = ("gcE", gs_)
                        B.dma("sp", GC[gs_][:, 0:256], GROW_S[:, cc * 256:(cc + 1) * 256], writes=[gk], dsem=gk)
                        for t in range(ntile):
                            pst, pk = MMR[B.ring("mmr", 4)]
                            for dc in range(DC):
                                B.mm(pst[:, 0:256], MGT[:, dc, t * 128:(t + 1) * 128], WOC[wc][:, dc, :], dc == 0, dc == DC - 1, ["MGT", wck], [pk])
                            ti = B.ring("tmE", 2)
                            B.tt("dve", TM[ti][:], pst[:, 0:256], GC[gs_][:, 0:256], ALU.mult, [pk, gk], [("tmE", ti)])
                            B.tt("pool", XT[t][:, cc * 256:(cc + 1) * 256], XT[t][:, cc * 256:(cc + 1) * 256], TM[ti][:], ALU.add, [("tmE", ti), ("xtE", t)], [("xtE", t)])
                    for t in range(ntile):
                        ns = B.ring("xnE", 2)
                        norm_tile(XT[t][:], ("xtE", t), 1, H2T, "H2T", t * 128, XN[ns], ("xnE", ns), B.ring("stat", 8))

            def ffn_in_chunk(f, WFI, ncols, pst, pk):
                ws = B.ring("wfiF", 3); wk = ("wfiF", ws)
                B.dma("sp", WFI[ws][:], WFI_S[f, :, :, :], writes=[wk], dsem=wk)
                for dc in range(DC):
                    B.mm(pst[:, 0:ncols], WFI[ws][:, dc, :], H2T[:, dc, 0:ncols], dc == 0, dc == DC - 1, [wk, "H2T"], [pk])

            phase_E(NSLOT)
            with contextlib.ExitStack() as e1:
                WFI = [B.sb(e1, "WFIh%d" % i, [128, DC, 128], BF16) for i in range(3)]
                for f in range(2 * FC):
                    pst, pk = MMR[B.ring("mmr", 4)]
                    ffn_in_chunk(f, WFI, 16, pst, pk)
                    B.tt("dve", UH[:, f, :], pst[:, 0:16], HVAL[:], ALU.mult, [pk, "HVAL"], ["UH"])
            P.consts.add("UH")
            for s_ in range(nslot_f):
                phase_E(s_)
                with contextlib.ExitStack() as e1:
                    WFI = [B.sb(e1, "WFIf%d" % i, [128, DC, 128], BF16) for i in range(3)]
                    GT = B.sb(e1, "GTf", [128, FC, 512], BF16)
                    UB = [B.sb(e1, "UBf%d" % i, [128, 516], F32) for i in range(4)]
                    CAB = [B.sb(e1, "CABf%d" % i, [128, 512], F32) for i in range(4)]
                    SA = [B.sb(e1, "SAf%d" % i, [128, 512], F32) for i in range(2)]
                    WFO = [B.sb(e1, "WFOf%d" % i, [128, 11, 512], BF16) for i in range(3)]
                    OST = [B.sb(e1, "OSTf%d" % i, [128, 512], F32) for i in range(3)]
                    TMF = [B.sb(e1, "TMf%d" % i, [128, 512], F32) for i in range(2)]
                    for fp in range(FC):
                        cabs = []
                        for f in (fp, fp + FC):
                            pst, pk = MMR[B.ring("mmr", 4)]
                            ffn_in_chunk(f, WFI, 512, pst, pk)
                            ui = B.ring("ubF", 4); uk = ("ubF", ui)
                            B.cp("act", UB[ui][:, 2:514], pst[:, :], [pk], [uk])
                            B.cp("pool", UB[ui][:, 0:2], UH[:, f, 2 * s_:2 * s_ + 2], ["UH"], [uk])
                            ci = B.ring("cabF", 4); ck_ = ("cabF", ci)
                            B.ts("dve", CAB[ci][:], UB[ui][:, 2:514], CONVW[:, 2, f:f + 1], CONVB[:, f:f + 1], ALU.mult, ALU.add, [uk, "CONVW", "CONVB"], [ck_])
                            B.stt(CAB[ci][:], UB[ui][:, 1:513], CONVW[:, 1, f:f + 1], CAB[ci][:], ALU.mult, ALU.add, [uk, ck_, "CONVW"], [ck_])
                            B.stt(CAB[ci][:], UB[ui][:, 0:512], CONVW[:, 0, f:f + 1], CAB[ci][:], ALU.mult, ALU.add, [uk, ck_, "CONVW"], [ck_])
                            cabs.append((ci, ck_))
                        si = B.ring("saF", 2); sk_ = ("saF", si)
                        B.act(SA[si][:], CAB[cabs[0][0]][:], AF.Silu, [cabs[0][1]], [sk_])
                        B.tt("pool", GT[:, fp, :], SA[si][:], CAB[cabs[1][0]][:], ALU.mult, [sk_, cabs[1][1]], ["GTf"])
                    for cc in range(4):
                        gs_ = B.ring("gcE", 2); gk = ("gcE", gs_)
                        B.dma("sp", GC[gs_][:], GROW_S[:, D + cc * 512:D + (cc + 1) * 512], writes=[gk], dsem=gk)
                        for hf in range(4):
                            wo_ = B.ring("wfoF", 3); wok = ("wfoF", wo_)
                            B.dma("sp", WFO[wo_][:], wfo_v[:, hf * 11:(hf + 1) * 11, cc * 512:(cc + 1) * 512], writes=[wok], dsem=wok)
                            for t in range(4):
                                pst, pk = MMR[t]
                                for ki in range(11):
                                    kc_ = hf * 11 + ki
                                    B.mm(pst[:, :], GT[:, kc_, t * 128:(t + 1) * 128], WFO[wo_][:, ki, :], kc_ == 0, kc_ == FC - 1, ["GTf", wok], [pk])
                        for t in range(4):
                            pst, pk = MMR[t]
                            ti = B.ring("tmF", 2)
                            B.tt("dve", TMF[ti][:], pst[:, :], GC[gs_][:], ALU.mult, [pk, gk], [("tmF", ti)])
                            oi = B.ring("ostF", 3); ok_ = ("ostF", oi)
                            B.tt("pool", OST[oi][:], TMF[ti][:], XT[t][:, cc * 512:(cc + 1) * 512], ALU.add, [("tmF", ti), ("xtE", t)], [ok_])
                            B.dma("pq", out[s_ * 512 + t * 128:s_ * 512 + (t + 1) * 128, cc * 512:(cc + 1) * 512], OST[oi][:], reads=[ok_], dsem=ok_)
        P.barrier_dma("sp")
        P.emit()
        if os.environ.get("MK_VERBOSE"):
            print("n_ins", P.n_ins, {k: len(v) for k, v in P.streams.items()}, "nsem", len(P.phys_total), "arena_peak", B.arena_peak, flush=True)
    return dbg_out


def host_inputs(inp, core):
    b, j = core // 4, core % 4
    f32 = np.float32
    x = inp["x"]
    groups = _own_groups(j)
    o_par = (j, 3 - j)
    xo = np.zeros((NTOK, D), f32)
    for s, g in enumerate(groups):
        xo[s * 512:(s + 1) * 512] = x[b, g * 512:(g + 1) * 512]
        if g > 0:
            xo[NSLOT * 512 + 2 * s:NSLOT * 512 + 2 * s + 2] = x[b, g * 512 - 2:g * 512]
    col = lambda v: np.ascontiguousarray(np.asarray(v, f32).reshape(-1, 128).T)
    gains = np.stack([inp["q_norm_fox"][0] * SCALE, inp["k_norm_fox"][0], inp["q_norm_dsa"][0] * SCALE, inp["k_norm_dsa"][0],
                      inp["kv_norm_g"][0][:128], inp["kv_norm_g"][0][128:]], axis=1).astype(f32)
    cval = np.zeros((128, 8), f32); selo = np.zeros((128, 8), f32)
    for par in range(2):
        for c in range(4):
            cval[:, par * 4 + c] = 512 * o_par[par] + 128 * c
            selo[:, par * 4 + c] = 1.0 if o_par[par] == c else 0.0
    pos = np.zeros(128, f32); hval = np.zeros((128, 16), f32)
    hselt = np.zeros((128, NSLOT, 4, 16), f32)
    for s, g in enumerate(groups):
        for e in range(2):
            if g > 0:
                pos[2 * s + e] = 512 * g - 2 + e
                hval[:, 2 * s + e] = 1.0
                hselt[126 + e, s, o_par[s % 2], 2 * s + e] = 1.0
    hp2 = (np.arange(128, dtype=f32) - pos).reshape(128, 1)
    hposr = np.broadcast_to(pos[None, :16], (128, 16)).astype(f32)
    convw = inp["conv_w"][0].reshape(3, 2 * FC, 128).transpose(2, 0, 1)
    return {
        "xb": np.ascontiguousarray(x[b]), "xo": xo, "cT": col(inp["c"][b]),
        "w_ada": inp["w_ada"][0], "b_ada_c": col(inp["b_ada"][0]), "b_ada_r": inp["b_ada"][0].reshape(1, -1),
        "n1c": col(inp["norm1_g"][0]), "n2c": col(inp["norm2_g"][0]), "w_in": inp["w_in"][0],
        "b_forget": inp["b_forget"][0].reshape(1, 8), "gains": np.ascontiguousarray(gains), "w_ukv": inp["w_ukv"][0],
        "w_out_fox": inp["w_out_fox"][0], "w_out_dsa": inp["w_out_dsa"][0], "w_out": inp["w_out"][0],
        "w_ffn_in": inp["w_ffn_in"][0], "convw": np.ascontiguousarray(convw, dtype=f32), "convb": col(inp["conv_b"][0]),
        "w_ffn_out": inp["w_ffn_out"][0], "relb": inp["rel_bias"].reshape(1, 256),
        "cval": cval, "selo": selo, "selv": np.zeros((128, 64), f32), "hp2": hp2, "hposr": hposr,
        "hselt": np.ascontiguousarray(hselt.reshape(128, 512)), "hval": hval,
    }


def kernel(**inputs):
    inp = {k: np.asarray(v) for k, v in inputs.items()}
    nc = bass.Bass("TRN2", target_bir_lowering=False)
    build_program(nc)
    in_maps = [host_inputs(inp, c) for c in range(8)]
    res = run_bass_kernel_spmd(nc, in_maps, core_ids=list(range(8)))
    out = np.zeros((2, L, D), np.float32)
    for c in range(8):
        b, j = c // 4, c % 4
        o = np.asarray(res.results[c]["out"])
        for s, g in enumerate(_own_groups(j)):
            out[b, g * 512:(g + 1) * 512] = o[s * 512:(s + 1) * 512]
    return out
```
